# Optimizing a Trainium2 kernel written in Bass

```python
import jax, jax.numpy as jnp
from jax import lax
import numpy as np

D_MODEL = 1024
BATCH = 8
SEQ = 2048
DEPTH = 1

GRID_W = 64
CTX_LEN = 256
RET_HEADS = 4
RET_DK = 256
RET_DV = 256
RET_QK = RET_HEADS * RET_DK
RET_V = RET_HEADS * RET_DV
RET_CHUNK = 128
MLP_WIDTH = 1024
MLP_GROUPS = 8
MLP_GROUP_DIM = MLP_WIDTH // MLP_GROUPS
MLP_CHUNK = 128
ROPE_BASE = 10000.0
EPS = 1e-6
IN_SIZES = (RET_QK, RET_QK, RET_V, RET_V, MLP_WIDTH, MLP_WIDTH, MLP_WIDTH, D_MODEL, D_MODEL)
IN_WIDTH = sum(IN_SIZES)

kernel_name = "hybrid_retention_chunkmlp_prefix_block"


def rms_norm(x, g):
    xf = x.astype(jnp.float32)
    y = xf * lax.rsqrt(jnp.mean(xf * xf, axis=-1, keepdims=True) + EPS)
    return (y * g.astype(jnp.float32)).astype(x.dtype)


def layer_norm(x, g, b):
    xf = x.astype(jnp.float32)
    mu = jnp.mean(xf, axis=-1, keepdims=True)
    xc = xf - mu
    y = xc * lax.rsqrt(jnp.mean(xc * xc, axis=-1, keepdims=True) + EPS)
    return (y * g.astype(jnp.float32) + b.astype(jnp.float32)).astype(x.dtype)


def ada_modulation(cond, w_mod, b_mod):
    m = jax.nn.silu(cond) @ w_mod + b_mod
    return jnp.split(m, 3, axis=-1)


def split_columns(p):
    out, start = [], 0
    for size in IN_SIZES:
        out.append(p[..., start:start + size])
        start += size
    return out


def to_heads(t, d):
    b, n, _ = t.shape
    return t.reshape(b, n, RET_HEADS, d).transpose(0, 2, 1, 3)


def axial_rope(t):
    n = t.shape[2]
    rows = n // GRID_W
    r = jnp.broadcast_to(jnp.arange(rows, dtype=jnp.float32)[:, None], (rows, GRID_W)).reshape(n)
    col = jnp.broadcast_to(jnp.arange(GRID_W, dtype=jnp.float32)[None, :], (rows, GRID_W)).reshape(n)
    half = t.shape[-1] // 2
    quarter = half // 2
    inv = jnp.power(ROPE_BASE, -jnp.arange(quarter, dtype=jnp.float32) / quarter)
    ang = jnp.concatenate([r[:, None] * inv, col[:, None] * inv], axis=-1)
    cos, sin = jnp.cos(ang), jnp.sin(ang)
    t1, t2 = t[..., :half], t[..., half:]
    return jnp.concatenate([t1 * cos - t2 * sin, t1 * sin + t2 * cos], axis=-1)


def retention_scan(q, k, v, log_gamma, init_state, include_diag):
    b, h, n, dk = q.shape
    dv = v.shape[-1]
    C = RET_CHUNK
    nc = n // C
    qc = q.reshape(b, h, nc, C, dk)
    kc = k.reshape(b, h, nc, C, dk)
    vc = v.reshape(b, h, nc, C, dv)
    idx = jnp.arange(C, dtype=jnp.float32)
    diff = idx[:, None] - idx[None, :]
    mask = diff >= 0 if include_diag else diff > 0
    lg = log_gamma[:, None, None]
    dmat = jnp.where(mask[None], jnp.exp(lg * jnp.where(mask, diff, 0.0)[None]), 0.0)
    scores = jnp.einsum('bhcid,bhcjd->bhcij', qc, kc) * dmat[None, :, None]
    intra = jnp.einsum('bhcij,bhcjv->bhciv', scores, vc)
    k_decay = jnp.exp(log_gamma[:, None] * (C - 1 - idx)[None])
    chunk_kv = jnp.einsum('bhcjd,bhcjv->cbhdv', kc * k_decay[None, :, None, :, None], vc)
    chunk_decay = jnp.exp(log_gamma * C)[None, :, None, None]

    def step(state, kv):
        return chunk_decay * state + kv, state

    _, starts = lax.scan(step, init_state, chunk_kv)
    q_decay = jnp.exp(log_gamma[:, None] * (idx + 1.0)[None])
    cross = jnp.einsum('bhcid,cbhdv->bhciv', qc, starts) * q_decay[None, :, None, :, None]
    return (intra + cross).reshape(b, h, n, dv)


def context_states(k, v, lg_f, lg_b):
    n = k.shape[2]
    pos = jnp.arange(n, dtype=jnp.float32)
    w_f = jnp.exp(lg_f[:, None] * (n - 1 - pos)[None])
    w_b = jnp.exp(lg_b[:, None] * pos[None])
    s_f = jnp.einsum('bhnd,hn,bhnv->bhdv', k, w_f, v)
    s_b = jnp.einsum('bhnd,hn,bhnv->bhdv', k, w_b, v)
    return s_f, s_b


def head_group_norm(y, g):
    mu = jnp.mean(y, axis=-1, keepdims=True)
    yc = y - mu
    yn = yc * lax.rsqrt(jnp.mean(yc * yc, axis=-1, keepdims=True) + EPS)
    b, h, n, d = y.shape
    return yn.transpose(0, 2, 1, 3).reshape(b, n, h * d) * g.astype(jnp.float32)


def chunk_spatial_gating(u, v_g, ln_g, ln_b, ws, bs):
    u = jax.nn.gelu(u)
    v_g = layer_norm(jax.nn.gelu(v_g), ln_g, ln_b)
    b, n, w = v_g.shape
    nc = n // MLP_CHUNK
    vc = v_g.reshape(b, nc, MLP_CHUNK, MLP_GROUPS, MLP_GROUP_DIM)
    mixed = jnp.einsum('gpq,bcqgd->bcpgd', ws, vc) + bs.T[None, None, :, :, None]
    return u * mixed.reshape(b, n, w)


def mixer_branches(parts, on_grid, state_f, state_b, lg_f, lg_b, ret_norm_g,
                   mlp_ln_g, mlp_ln_b, mlp_ws, mlp_bs, w_proj_a, w_proj_b, w_out):
    q, k, v, z_a, u, v_g, z_b, g_a, g_b = parts
    dtype = u.dtype
    qh = to_heads(q, RET_DK).astype(jnp.float32) * (RET_DK ** -0.5)
    kh = to_heads(k, RET_DK).astype(jnp.float32)
    vh = to_heads(v, RET_DV).astype(jnp.float32)
    if on_grid:
        qh, kh = axial_rope(qh), axial_rope(kh)
    ret_f = retention_scan(qh, kh, vh, lg_f, state_f, True)
    ret_b = retention_scan(qh[:, :, ::-1], kh[:, :, ::-1], vh[:, :, ::-1], lg_b, state_b, False)[:, :, ::-1]
    y_a = head_group_norm(ret_f + ret_b, ret_norm_g).astype(dtype) * jax.nn.silu(z_a)
    y_b = chunk_spatial_gating(u, v_g, mlp_ln_g, mlp_ln_b, mlp_ws, mlp_bs) * jax.nn.silu(z_b)
    merged = jax.nn.sigmoid(g_a) * (y_a @ w_proj_a) + jax.nn.sigmoid(g_b) * (y_b @ w_proj_b)
    return merged @ w_out


def setup_inputs(seed: int = 0) -> dict:
    key = jax.random.key(seed)
    ks = jax.random.split(key, 20)
    f32 = jnp.float32
    nrm = lambda k, s, sc: jax.random.normal(k, s, f32) * sc
    base_gamma = 1.0 - jnp.power(2.0, -5.0 - jnp.arange(RET_HEADS, dtype=f32))
    base_logit = jnp.log(base_gamma) - jnp.log1p(-base_gamma)
    return {
        'x': nrm(ks[0], (BATCH, SEQ, D_MODEL), 1.0),
        'c': nrm(ks[1], (BATCH, D_MODEL), 1.0),
        'ctx': nrm(ks[2], (BATCH, CTX_LEN, D_MODEL), 1.0),
        'c_ctx': nrm(ks[3], (D_MODEL,), 1.0),
        'w_mod': nrm(ks[4], (DEPTH, D_MODEL, 3 * D_MODEL), 0.5 * D_MODEL ** -0.5),
        'b_mod': nrm(ks[5], (DEPTH, 3 * D_MODEL), 0.01),
        'norm_g': 1.0 + nrm(ks[6], (DEPTH, D_MODEL), 0.02),
        'w_in': nrm(ks[7], (DEPTH, D_MODEL, IN_WIDTH), D_MODEL ** -0.5),
        'ret_decay_fwd': base_logit[None] + nrm(ks[8], (DEPTH, RET_HEADS), 0.1),
        'ret_decay_bwd': base_logit[None] + nrm(ks[9], (DEPTH, RET_HEADS), 0.1),
        'ret_norm_g': 1.0 + nrm(ks[10], (DEPTH, RET_V), 0.02),
        'mlp_ln_g': 1.0 + nrm(ks[11], (DEPTH, MLP_WIDTH), 0.02),
        'mlp_ln_b': nrm(ks[12], (DEPTH, MLP_WIDTH), 0.02),
        'mlp_ws': nrm(ks[13], (DEPTH, MLP_GROUPS, MLP_CHUNK, MLP_CHUNK), MLP_CHUNK ** -0.5),
        'mlp_bs': 1.0 + nrm(ks[14], (DEPTH, MLP_GROUPS, MLP_CHUNK), 0.1),
        'w_proj_a': nrm(ks[15], (DEPTH, RET_V, D_MODEL), RET_V ** -0.5),
        'w_proj_b': nrm(ks[16], (DEPTH, MLP_WIDTH, D_MODEL), MLP_WIDTH ** -0.5),
        'w_out': nrm(ks[17], (DEPTH, D_MODEL, D_MODEL), D_MODEL ** -0.5),
        'final_norm_g': 1.0 + nrm(ks[18], (D_MODEL,), 0.02),
    }


def reference(x, c, ctx, c_ctx, w_mod, b_mod, norm_g, w_in, ret_decay_fwd, ret_decay_bwd,
              ret_norm_g, mlp_ln_g, mlp_ln_b, mlp_ws, mlp_bs, w_proj_a, w_proj_b, w_out,
              final_norm_g):
    for l in range(DEPTH):
        last = l == DEPTH - 1
        sh_x, sc_x, gt_x = ada_modulation(c[:, None, :], w_mod[l], b_mod[l])
        sh_c, sc_c, gt_c = ada_modulation(c_ctx[None, None, :], w_mod[l], b_mod[l])
        hx = rms_norm(x, norm_g[l]) * (1.0 + sc_x) + sh_x
        hc = rms_norm(ctx, norm_g[l]) * (1.0 + sc_c) + sh_c
        lg_f = jax.nn.log_sigmoid(ret_decay_fwd[l].astype(jnp.float32))
        lg_b = jax.nn.log_sigmoid(ret_decay_bwd[l].astype(jnp.float32))
        if last:
            kv_c = hc @ w_in[l][:, RET_QK:2 * RET_QK + RET_V]
            kc_h = to_heads(kv_c[..., :RET_QK], RET_DK).astype(jnp.float32)
            vc_h = to_heads(kv_c[..., RET_QK:], RET_DV).astype(jnp.float32)
        else:
            pc = split_columns(hc @ w_in[l])
            kc_h = to_heads(pc[1], RET_DK).astype(jnp.float32)
            vc_h = to_heads(pc[2], RET_DV).astype(jnp.float32)
        s_f, s_b = context_states(kc_h, vc_h, lg_f, lg_b)
        px = split_columns(hx @ w_in[l])
        out_x = mixer_branches(px, True, s_f, s_b, lg_f, lg_b, ret_norm_g[l], mlp_ln_g[l], mlp_ln_b[l],
                               mlp_ws[l], mlp_bs[l], w_proj_a[l], w_proj_b[l], w_out[l])
        if not last:
            zeros = jnp.zeros_like(s_f)
            out_c = mixer_branches(pc, False, zeros, zeros, lg_f, lg_b, ret_norm_g[l], mlp_ln_g[l], mlp_ln_b[l],
                                   mlp_ws[l], mlp_bs[l], w_proj_a[l], w_proj_b[l], w_out[l])
            ctx = ctx + gt_c * out_c
        x = x + gt_x * out_x
    return rms_norm(x, final_norm_g)
```

```python
import contextlib
import numpy as np
import concourse.bass as bass
import concourse.mybir as mybir
from concourse.bass_utils import run_bass_kernel_spmd

F32 = mybir.dt.float32
BF16 = mybir.dt.bfloat16
AF = mybir.ActivationFunctionType
ALU = mybir.AluOpType

D = 1024
N = 2048
NCTX = 256
H = 4
C = 128
EPS = 1e-6
NT = N // 128
LN16 = float(np.log(1.0 / 16.0))


class _Stop(Exception):
    pass


ATTACH_WAITS = True


class _Rec:
    def __init__(self, e):
        self._e = e
        self.first = None

    def __getattr__(self, name):
        attr = getattr(self._e, name)
        if not callable(attr):
            return attr

        def w(*a, **k):
            r = attr(*a, **k)
            if self.first is None and r is not None:
                self.first = r
            return r
        return w


class Trk:
    def __init__(self, nc):
        self.nc = nc
        self.E = {}
        for name, e in (("pe", nc.tensor), ("act", nc.scalar), ("dve", nc.vector),
                        ("pool", nc.gpsimd), ("sp", nc.sync)):
            self.E[name] = dict(e=e, sem=nc.alloc_semaphore("s_" + name), cnt=0, known={}, mult=1)
        self.res = {}
        self.snap = {}
        self.nwaits = 0
        self.ntasks = 0
        self.limit = 0

    def dsem(self, name):
        pn = "d:" + name
        if pn not in self.E:
            self.E[pn] = dict(e=None, sem=self.nc.alloc_semaphore("sd_" + name), cnt=0, known={}, mult=16)
        return pn

    def _deps(self, en, reads, writes, strict=False):
        need = {}

        def add(dep, raw):
            n, s = dep
            if n == en and not raw and not strict:
                return
            if need.get(n, 0) < s:
                need[n] = s

        for k in reads:
            r = self.res.get(k)
            if r and r[0]:
                add(r[0], True)
        for k in writes:
            r = self.res.get(k)
            if r:
                if r[0]:
                    add(r[0], False)
                for d in r[1].items():
                    add(d, False)
        return need

    def _wait(self, en, need, defer_last=False):
        E = self.E[en]
        todo = []
        for n, s in sorted(need.items()):
            if E["known"].get(n, 0) >= s:
                continue
            src = self.E[n]
            todo.append((src["sem"], s * src["mult"]))
            self.nwaits += 1
            E["known"][n] = s
            sn = self.snap.get((n, s))
            if sn:
                for k2, v2 in sn.items():
                    if k2 != en and E["known"].get(k2, 0) < v2:
                        E["known"][k2] = v2
        last = None
        if defer_last and todo:
            last = todo.pop()
        for sem, val in todo:
            E["e"].wait_ge(sem, val)
        return last

    def _commit(self, prod, seq, reads, writes):
        for k in writes:
            self.res[k] = [(prod, seq), {}]
        for k in reads:
            r = self.res.setdefault(k, [None, {}])
            if r[1].get(prod, 0) < seq:
                r[1][prod] = seq

    def task(self, en, fn, reads=(), writes=()):
        need = self._deps(en, reads, writes, strict=(en == "pool"))
        pend = self._wait(en, need, defer_last=ATTACH_WAITS)
        E = self.E[en]
        if pend is not None:
            rec = _Rec(E["e"])
            last = fn(rec)
            rec.first._wait_ge(pend[0], pend[1])
        else:
            last = fn(E["e"])
        E["cnt"] += 1
        last.then_inc(E["sem"], 1)
        self.snap[(en, E["cnt"])] = dict(E["known"])
        self._commit(en, E["cnt"], reads, writes)
        self.ntasks += 1
        if self.limit and self.ntasks >= self.limit:
            raise _Stop()

    def dma(self, qn, dname, pairs, reads=(), writes=()):
        pn = self.dsem(dname)
        need = self._deps(qn, reads, writes, strict=True)
        P = self.E[pn]
        if P["cnt"] > 0:
            if need.get(pn, 0) < P["cnt"]:
                need[pn] = P["cnt"]
        self._wait(qn, need)
        Q = self.E[qn]
        for (o, i) in pairs:
            Q["e"].dma_start(out=o, in_=i).then_inc(P["sem"], 16)
            P["cnt"] += 1
        self.snap[(pn, P["cnt"])] = dict(Q["known"])
        self._commit(pn, P["cnt"], reads, writes)

    def barrier(self):
        names = [n for n in self.E]
        for en in ("pe", "act", "dve", "pool", "sp"):
            need = {n: self.E[n]["cnt"] for n in names if self.E[n]["cnt"] > 0}
            self._wait(en, need)

    def finish(self, en="sp"):
        need = {n: self.E[n]["cnt"] for n in self.E if n != en and self.E[n]["cnt"] > 0}
        self._wait(en, need)


def build(taps=None, stop=None):
    taps = taps or set()

    def ck(name):
        if stop == name:
            raise _Stop()
    nc = bass.Bass("TRN2", target_bir_lowering=False)
    T = Trk(nc)
    if isinstance(stop, int):
        T.limit = stop

    def dram_in(name, shape):
        return nc.dram_tensor(name, list(shape), F32, kind="ExternalInput").ap()

    x_d = dram_in("x", [N, D])
    ctx_d = dram_in("ctx", [NCTX, D])
    cT_d = dram_in("cT", [128, 8, 2])
    wmod_d = dram_in("wmod", [128, 8, 3 * D])
    colv_d = dram_in("colv", [128, 64])
    rowA_d = dram_in("rowA", [1, 264])
    bmod2_d = dram_in("bmod2", [2, 3 * D])
    fg_d = dram_in("fg", [1, D])
    lnb_d = dram_in("lnb", [1, D])
    bs_d = dram_in("bs", [1, 8 * 128])
    cmat_d = dram_in("cmat", [128, 4, 128])
    rope_d = dram_in("rope", [128, 2, N])
    ident_d = dram_in("ident", [128, 128])
    wsT_d = dram_in("wsT", [128, 8, 128])
    wh_d = dram_in("wh", [H, 128, 4, 8, 256])
    wB_d = dram_in("wB", [3, 128, 8, D])
    w4_d = dram_in("w4", [8, 128, 4, 8, 128])
    wout_d = dram_in("wout", [128, 8, D])
    y_d = nc.dram_tensor("y", [N, D], F32, kind="ExternalOutput").ap()

    tap_out = {}

    ps = [nc.alloc_psum_tensor(f"ps{i}", [128, 512], F32) for i in range(8)]

    def PS(i):
        return f"ps{i}"

    def psb(i):
        return ps[i][:].bitcast(BF16)

    outer = contextlib.ExitStack()

    def sb(stack, name, shape, dt):
        return stack.enter_context(nc.sbuf_tensor("sb_" + name, list(shape), dt))

    def tap(name, ap, shape, reads, is_bf16):
        if name not in taps:
            return
        d = nc.dram_tensor("tap_" + name, list(shape), F32, kind="ExternalOutput").ap()
        tap_out[name] = d
        if is_bf16:
            T.dma("pool", "tap_" + name, [(d, ap)], reads=reads)
        else:
            T.dma("sp", "tap_" + name, [(d, ap)], reads=reads)

    def mm(out_ap, pairs, reads, writes, extra=None):
        def fn(pe):
            n = len(pairs)
            ins = None
            for i, (l, r) in enumerate(pairs):
                ins = pe.matmul(out_ap, l, r, start=(i == 0), stop=(i == n - 1))
            return ins
        T.task("pe", fn, reads, writes)

    def mm_multi(groups, reads, writes):
        def fn(pe):
            ins = None
            for out_ap, pairs in groups:
                n = len(pairs)
                for i, (l, r) in enumerate(pairs):
                    ins = pe.matmul(out_ap, l, r, start=(i == 0), stop=(i == n - 1))
            return ins
        T.task("pe", fn, reads, writes)

    try:
      with outer:
          hxT = sb(outer, "hxT", [128, 8, N], BF16)
          yaT = sb(outer, "yaT", [128, 8, N], BF16)
          colv = sb(outer, "colv", [128, 64], F32)
          mcol = sb(outer, "mcol", [128, 16, 2], F32)
          acol = sb(outer, "acol", [128, 2, 8], F32)
          gtb = sb(outer, "gtb", [128, D], F32)
          lg = sb(outer, "lg", [128, 8], F32)
          gC = sb(outer, "gC", [128, 8], F32)
          wh = sb(outer, "wh", [128, 4, 8, 256], BF16)
          whv = wh[:].rearrange("p a b c -> p (a b c)").rearrange("p (k n) -> p k n", k=8)
          p02 = contextlib.ExitStack()
          hcT = sb(p02, "hcT", [128, 8, NCTX], BF16)
          ident = sb(p02, "ident", [128, 128], BF16)
          rowA = sb(p02, "rowA", [128, 264], F32)
          cmat = sb(p02, "cmat", [128, 4, 128], F32)
          DT = sb(p02, "DT", [128, H, 128], F32)
          qdec = sb(p02, "qdec", [128, H, 2, 128], F32)
          kdec = sb(p02, "kdec", [128, 8], F32)
          cdec = sb(p02, "cdec", [128, 2, 8], F32)
          rope = sb(p02, "rope", [128, 2, N], F32)

          T.dma("sp", "c", [(colv[:], colv_d), (rowA[:], rowA_d.partition_broadcast(128)),
                            (cmat[:], cmat_d)], writes=["consts"])
          T.dma("pool", "c2", [(ident[:], ident_d)], writes=["ident"])

          with contextlib.ExitStack() as p0:
              cT = sb(p0, "cT", [128, 8, 2], F32)
              sc = sb(p0, "sc", [128, 8, 2], BF16)
              wm = [sb(p0, f"wm{i}", [128, 8, 512], BF16) for i in range(3)]
              bmod2 = sb(p0, "bmod2", [2, 3 * D], F32)
              mrow = sb(p0, "mrow", [2, 3 * D], F32)
              id2 = sb(p0, "id2", [2, 2], F32)
              ones = sb(p0, "ones", [1, 128], F32)
              tmp8 = sb(p0, "tmp8", [128, 2, 8], F32)
              tmpE = sb(p0, "tmpE", [128, H, 2, 128], F32)

              T.dma("sp", "c", [(cT[:], cT_d), (bmod2[:], bmod2_d), (id2[:], ident_d[0:2, 0:2])],
                    writes=["consts2"])
              T.task("pool", lambda e: e.memset(ones[:], 1.0), writes=["ones"])
              T.task("act", lambda e: e.activation(out=sc[:], in_=cT[:], func=AF.Silu),
                     reads=["consts2"], writes=["sc"])

              def mod_dma(blk):
                  sl = blk % 3
                  T.dma("pool", f"wm{sl}", [(wm[sl][:], wmod_d[:, :, blk * 512:(blk + 1) * 512])], writes=[f"wm{sl}"])

              def mod_block(blk):
                  sl = blk % 3
                  bank = 4 + blk % 4
                  mm(ps[bank][0:2, :], [(sc[:, kt, :], wm[sl][:, kt, :]) for kt in range(8)],
                     reads=[f"wm{sl}", "sc"], writes=[PS(bank)])
                  T.task("dve", lambda e: e.tensor_tensor(
                      out=mrow[0:2, blk * 512:(blk + 1) * 512], in0=ps[bank][0:2, :],
                      in1=bmod2[0:2, blk * 512:(blk + 1) * 512], op=ALU.add),
                      reads=[PS(bank), "consts2"], writes=["mrow"])
                  if blk + 3 < 6:
                      mod_dma(blk + 3)
                  if blk == 2:
                      T.dma("pool", "wh", [(wh[:, s_, :, :], wh_d[0, :, s_, :, :]) for s_ in range(4)], writes=["wh"])
              mod_dma(0)
              mod_dma(1)
              mod_dma(2)

              xs = [sb(p0, f"xs{i}", [128, D], F32) for i in range(4)]
              xb = [sb(p0, f"xb{i}", [128, D], BF16) for i in range(2)]
              junk = sb(p0, "junk", [128, D], BF16)
              ss = sb(p0, "ss", [128, 20], F32)
              lss = sb(p0, "lss", [128, 20], F32)
              rstd = sb(p0, "rstd", [128, 20], F32)
              T.task("pool", lambda e: e.memset(ss[:], 0.0), writes=["ss"])
              ntile = NT + NCTX // 128
              for tt in range(ntile):
                  s3 = tt % 4
                  s2 = tt % 2
                  src = x_d[tt * 128:(tt + 1) * 128, :] if tt < NT else ctx_d[(tt - NT) * 128:(tt - NT + 1) * 128, :]
                  T.dma("sp", f"xs{s3}", [(xs[s3][:], src)], writes=[f"xs{s3}"])
                  T.task("act", lambda e, s3=s3, tt=tt: e.activation(out=junk[:], in_=xs[s3][:], func=AF.Square,
                                                                       accum_out=ss[:, tt:tt + 1]),
                         reads=[f"xs{s3}", "ss"], writes=["junk", f"ss{tt}"])
                  T.task("act", lambda e, tt=tt: e.activation(out=lss[:, tt:tt + 1], in_=ss[:, tt:tt + 1], func=AF.Ln,
                                                                scale=1.0 / D, bias=EPS),
                         reads=[f"ss{tt}"], writes=[f"lss{tt}"])
                  T.task("act", lambda e, tt=tt: e.activation(out=rstd[:, tt:tt + 1], in_=lss[:, tt:tt + 1],
                                                                func=AF.Exp, scale=-0.5),
                         reads=[f"lss{tt}"], writes=[f"rstd{tt}"])
                  T.task("dve", lambda e, s3=s3, s2=s2, tt=tt: e.tensor_scalar(
                      out=xb[s2][:], in0=xs[s3][:], scalar1=rstd[:, tt:tt + 1], scalar2=None, op0=ALU.mult),
                      reads=[f"xs{s3}", f"rstd{tt}"], writes=[f"xb{s2}"])
                  bank = tt % 4

                  def tr(pe, s2=s2, bank=bank):
                      ins = None
                      for kt in range(8):
                          ins = pe.transpose(psb(bank)[:, kt * 128:(kt + 1) * 128], xb[s2][:, kt * 128:(kt + 1) * 128],
                                             ident[:])
                      return ins
                  T.task("pe", tr, reads=[f"xb{s2}", "ident"], writes=[PS(bank)])
                  if tt < NT:
                      dst = hxT[:, :, tt * 128:(tt + 1) * 128]
                      key = "hxT"
                  else:
                      dst = hcT[:, :, (tt - NT) * 128:(tt - NT + 1) * 128]
                      key = "hcT"
                  T.task("dve", lambda e, dst=dst, bank=bank: e.tensor_copy(
                      out=dst, in_=psb(bank).rearrange("p (k t) -> p k t", k=8)),
                      reads=[PS(bank)], writes=[key])
                  if tt % 3 == 2:
                      mod_block(tt // 3)

              def mtr(pe):
                  ins = None
                  for t in range(16):
                      ins = pe.transpose(ps[6][:, 2 * t:2 * t + 2], mrow[0:2, t * 128:(t + 1) * 128], id2[0:2, 0:2])
                  return ins
              T.task("pe", mtr, reads=["mrow", "consts2"], writes=[PS(6)])
              T.task("dve", lambda e: e.tensor_copy(out=mcol[:], in_=ps[6][:, 0:32].rearrange("p (t w) -> p t w", w=2)),
                     reads=[PS(6)], writes=["mcol"])

              def acol_build(e):
                  ins = None
                  for w in range(2):
                      ins = e.scalar_tensor_tensor(out=acol[:, w, :], in0=mcol[:, 8:16, w], scalar=1.0,
                                                   in1=colv[:, 24:32], op0=ALU.add, op1=ALU.mult)
                  return ins
              T.task("dve", acol_build, reads=["mcol", "consts"], writes=["acol"])
              for half in range(2):
                  mm(ps[7][:], [(ones[0:1, :], mrow[0:1, 2 * D + half * 512:2 * D + (half + 1) * 512])],
                     reads=["ones", "mrow"], writes=[PS(7)])
                  T.task("dve", lambda e, half=half: e.tensor_copy(out=gtb[:, half * 512:(half + 1) * 512],
                                                                   in_=ps[7][:]),
                         reads=[PS(7)], writes=["gtb"])
              tap("gtb", gtb[:], [128, D], ["gtb"], False)

              T.task("act", lambda e: e.activation(out=tmp8[:, 0, :], in_=rowA[:, 0:8], func=AF.Exp, scale=-1.0),
                     reads=["consts"], writes=["tmp8a"])
              T.task("act", lambda e: e.activation(out=tmp8[:, 1, :], in_=tmp8[:, 0, :], func=AF.Ln, bias=1.0),
                     reads=["tmp8a"], writes=["tmp8b"])
              T.task("dve", lambda e: e.tensor_scalar(out=lg[:], in0=tmp8[:, 1, :], scalar1=-1.0, scalar2=None,
                                                       op0=ALU.mult),
                     reads=["tmp8b"], writes=["lg"])
              T.task("act", lambda e: e.activation(out=gC[:], in_=lg[:], func=AF.Exp, scale=float(C)),
                     reads=["lg"], writes=["gC"])

              def dec_tables(e):
                  ins = None
                  for h in range(H):
                      lf = lg[:, h:h + 1]
                      lb = lg[:, 4 + h:5 + h]
                      e.activation(out=tmpE[:, h, 0, :], in_=cmat[:, 0, :], func=AF.Exp, scale=lf)
                      e.activation(out=tmpE[:, h, 1, :], in_=cmat[:, 1, :], func=AF.Exp, scale=lb)
                      e.activation(out=qdec[:, h, 0, :], in_=rowA[:, 8:136], func=AF.Exp, scale=lf, bias=LN16)
                      e.activation(out=qdec[:, h, 1, :], in_=rowA[:, 136:264], func=AF.Exp, scale=lb, bias=LN16)
                      e.activation(out=kdec[:, h:h + 1], in_=colv[:, 48:49], func=AF.Exp, scale=lf)
                      e.activation(out=kdec[:, 4 + h:5 + h], in_=colv[:, 49:50], func=AF.Exp, scale=lb)
                      for j in range(2):
                          e.activation(out=cdec[:, j, h:h + 1], in_=colv[:, 50 + j:51 + j], func=AF.Exp, scale=lf)
                          ins = e.activation(out=cdec[:, j, 4 + h:5 + h], in_=colv[:, 52 + j:53 + j],
                                             func=AF.Exp, scale=lb)
                  return ins
              T.task("act", dec_tables, reads=["lg", "consts"], writes=["tmpE", "dectabs"])

              def dt_build(e):
                  ins = None
                  for h in range(H):
                      e.tensor_tensor(out=tmpE[:, h, 0, :], in0=tmpE[:, h, 0, :], in1=cmat[:, 2, :], op=ALU.mult)
                      ins = e.tensor_tensor(out=tmpE[:, h, 1, :], in0=tmpE[:, h, 1, :], in1=cmat[:, 3, :],
                                            op=ALU.mult)
                  return ins
              T.task("dve", dt_build, reads=["tmpE", "consts"], writes=["tmpE2"])
              T.task("dve", lambda e: e.tensor_tensor(out=DT[:], in0=tmpE[:, :, 0, :], in1=tmpE[:, :, 1, :],
                                                       op=ALU.add),
                     reads=["tmpE2"], writes=["DT"])

              def affine_x(e):
                  ins = None
                  for kt in range(8):
                      ins = e.tensor_scalar(out=hxT[:, kt, :], in0=hxT[:, kt, :], scalar1=acol[:, 0, kt:kt + 1],
                                            scalar2=mcol[:, kt, 0:1], op0=ALU.mult, op1=ALU.add)
                  return ins

              def affine_c(e):
                  ins = None
                  for kt in range(8):
                      ins = e.tensor_scalar(out=hcT[:, kt, :], in0=hcT[:, kt, :], scalar1=acol[:, 1, kt:kt + 1],
                                            scalar2=mcol[:, kt, 1:2], op0=ALU.mult, op1=ALU.add)
                  return ins
              T.task("dve", affine_c, reads=["acol", "mcol", "hcT"], writes=["hcT"])
              T.task("dve", affine_x, reads=["acol", "mcol", "hxT"], writes=["hxT"])
              tap("hxT", hxT[:], [128, 8, N], ["hxT"], True)
              tap("DT", DT[:], [128, H, 128], ["DT"], False)
              T.dma("sp", "rope", [(rope[:], rope_d)], writes=["rope"])
              ck("p1")
              T.barrier()

          with contextlib.ExitStack() as p2:
              qT = sb(p2, "qT", [128, 2, N], BF16)
              kT = sb(p2, "kT", [128, 2, N], BF16)
              kf = sb(p2, "kf", [128, NT, 256], BF16)
              kb = sb(p2, "kb", [128, NT, 256], BF16)
              vv = sb(p2, "vv", [128, NT, 256], BF16)
              zaT = sb(p2, "zaT", [128, 2, N], BF16)
              rtmp = [sb(p2, f"rtmp{i}", [128, 4, 256], F32) for i in range(2)]
              Tb = sb(p2, "Tb", [128, NT, 2, 256], BF16)
              S32 = sb(p2, "S32", [128, 2, 2, 2, 256], F32)
              Sf = [sb(p2, f"Sf{i}", [128, 2, 256], BF16) for i in range(2)]
              qfb = [sb(p2, f"qfb{i}", [128, 2, 2, 128], BF16) for i in range(2)]
              PT = [sb(p2, f"PT{i}", [128, 128], BF16) for i in range(2)]
              ynb = [sb(p2, f"ynb{i}", [128, 4, 256], BF16) for i in range(2)]
              kcf = sb(p2, "kcf", [128, 2, 256], BF16)
              kcb = sb(p2, "kcb", [128, 2, 256], BF16)
              vc = sb(p2, "vc", [128, 2, 256], BF16)
              gst = sb(p2, "gst", [128, 4, 6], F32)
              gmv = sb(p2, "gmv", [128, 4, 2], F32)
              gl = sb(p2, "gl", [128, 4], F32)
              grs = sb(p2, "grs", [128, 4], F32)
              gnm = sb(p2, "gnm", [128, 4], F32)

              for h in range(H):
                  def ctx_part(h=h):
                      for j in range(2):
                          bank = 4 + j
                          tok = slice(j * 128, (j + 1) * 128)
                          mm_multi([(ps[bank][:, 0:256], [(hcT[:, kt, tok], wh[:, 1, kt, :]) for kt in range(8)]),
                                    (ps[bank][:, 256:512], [(hcT[:, kt, tok], wh[:, 2, kt, :]) for kt in range(8)])],
                                   reads=["hcT", "wh"], writes=[PS(bank)])
                          ck("a0b")

                          def cev(e, j=j, bank=bank, h=h):
                              e.activation(out=kcf[:, j, :], in_=ps[bank][:, 0:256], func=AF.Copy,
                                           scale=cdec[:, j, h:h + 1])
                              return e.activation(out=kcb[:, j, :], in_=ps[bank][:, 0:256], func=AF.Copy,
                                                  scale=cdec[:, j, 4 + h:5 + h])
                          T.task("act", cev, reads=[PS(bank), "dectabs"], writes=["kc"])
                          ck("a0c")
                          T.task("act", lambda e, j=j, bank=bank: e.activation(out=vc[:, j, :], in_=ps[bank][:, 256:512],
                                                                                 func=AF.Copy),
                                 reads=[PS(bank)], writes=["vc"])
                      for dr, kc in ((0, kcf), (1, kcb)):
                          bank = 6 + dr
                          mm_multi([(ps[bank][:, dt * 256:(dt + 1) * 256],
                                     [(kc[:, j, dt * 128:(dt + 1) * 128], vc[:, j, :]) for j in range(2)])
                                    for dt in range(2)],
                                   reads=["kc", "vc"], writes=[PS(bank)])
                          T.task("dve", lambda e, dr=dr, bank=bank: e.tensor_copy(
                              out=S32[:, dr, 0, :, :], in_=ps[bank][:].rearrange("p (a b) -> p a b", a=2)),
                              reads=[PS(bank)], writes=[f"S32_{dr}_0"])
                      ck("a1")


                  def qk_unit(s, tb, ba, h=h):
                      dst, dkey = ((qT, "qT"), (kT, f"kT{tb}"))[s]
                      toks = slice(tb * 512, (tb + 1) * 512)
                      bb = ba + 1
                      mm(ps[ba][:], [(wh[:, s, kt, 0:128], hxT[:, kt, toks]) for kt in range(8)],
                         reads=["wh", "hxT"], writes=[PS(ba)])
                      mm(ps[bb][:], [(wh[:, s, kt, 128:256], hxT[:, kt, toks]) for kt in range(8)],
                         reads=["wh", "hxT"], writes=[PS(bb)])
                      for hf in range(2):
                          rt = rtmp[hf]
                          lo = tb * 512 + hf * 256
                          cs = rope[:, 0, lo:lo + 256]
                          sn = rope[:, 1, lo:lo + 256]
                          pa = ps[ba][:, hf * 256:(hf + 1) * 256]
                          pb = ps[bb][:, hf * 256:(hf + 1) * 256]

                          def rmul(e, rt=rt, cs=cs, sn=sn, pa=pa, pb=pb):
                              e.tensor_tensor(out=rt[:, 0, :], in0=pa, in1=cs, op=ALU.mult)
                              e.tensor_tensor(out=rt[:, 1, :], in0=pb, in1=sn, op=ALU.mult)
                              e.tensor_tensor(out=rt[:, 2, :], in0=pa, in1=sn, op=ALU.mult)
                              return e.tensor_tensor(out=rt[:, 3, :], in0=pb, in1=cs, op=ALU.mult)
                          T.task("dve", rmul, reads=[PS(ba), PS(bb), "rope"], writes=[f"rtmp{hf}"])

                          def radd(e, rt=rt, dst=dst, lo=lo):
                              e.tensor_tensor(out=dst[:, 0, lo:lo + 256], in0=rt[:, 0, :], in1=rt[:, 1, :],
                                              op=ALU.subtract)
                              return e.tensor_tensor(out=dst[:, 1, lo:lo + 256], in0=rt[:, 2, :], in1=rt[:, 3, :],
                                                     op=ALU.add)
                          T.task("pool", radd, reads=[f"rtmp{hf}"], writes=[dkey])

                  def ktr_group(g4, h=h):
                      bank = 4 + g4 % 2

                      def ktr(pe):
                          ins = None
                          for t in range(4):
                              tt = g4 * 4 + t
                              for dt in range(2):
                                  ins = pe.transpose(psb(bank)[:, t * 256 + dt * 128:t * 256 + (dt + 1) * 128],
                                                     kT[:, dt, tt * 128:(tt + 1) * 128], ident[:])
                          return ins
                      T.task("pe", ktr, reads=[f"kT{g4}", "ident"], writes=[PS(bank)])
                      pv = psb(bank).rearrange("p (a b) -> p a b", a=4)
                      T.task("act", lambda e: e.activation(
                          out=kf[:, g4 * 4:(g4 + 1) * 4, :], in_=pv, func=AF.Copy, scale=kdec[:, h:h + 1]),
                          reads=[PS(bank), "dectabs"], writes=["kf", PS(bank)])
                      T.task("dve", lambda e: e.tensor_scalar(
                          out=kb[:, g4 * 4:(g4 + 1) * 4, :], in0=pv, scalar1=kdec[:, 4 + h:5 + h], scalar2=None,
                          op0=ALU.mult),
                          reads=[PS(bank), "dectabs"], writes=["kb"])

                  def vproj(tb):
                      toks = slice(tb * 512, (tb + 1) * 512)
                      for dt in range(2):
                          bank = 6 + dt
                          mm(ps[bank][:], [(wh[:, 2, kt, dt * 128:(dt + 1) * 128], hxT[:, kt, toks]) for kt in range(8)],
                             reads=["wh", "hxT"], writes=[PS(bank)])
                          T.task("act", lambda e, dt=dt, bank=bank: e.activation(
                              out=zaT[:, dt, toks], in_=ps[bank][:], func=AF.Copy),
                              reads=[PS(bank)], writes=[f"zaT{tb}"])

                  def vtr_group(g4):
                      bank = 4 + g4 % 2

                      def vtr(pe):
                          ins = None
                          for t in range(4):
                              tt = g4 * 4 + t
                              for dt in range(2):
                                  ins = pe.transpose(psb(bank)[:, t * 256 + dt * 128:t * 256 + (dt + 1) * 128],
                                                     zaT[:, dt, tt * 128:(tt + 1) * 128], ident[:])
                          return ins
                      T.task("pe", vtr, reads=[f"zaT{g4}", "ident"], writes=[PS(bank)])
                      pv = psb(bank).rearrange("p (a b) -> p a b", a=4)
                      T.task("dve", lambda e: e.tensor_copy(out=vv[:, g4 * 4:(g4 + 1) * 4, :], in_=pv),
                             reads=[PS(bank)], writes=["vv"])

                  for tb in range(4):
                      qk_unit(1, tb, 2 * ((tb + 1) % 2))
                      if tb >= 1:
                          ktr_group(tb - 1)
                      if tb == 1:
                          ctx_part()
                  ck("a3")
                  vproj(0)
                  ktr_group(3)
                  for tb in range(1, 4):
                      vproj(tb)
                      vtr_group(tb - 1)
                  vtr_group(3)
                  ck("a4")

                  za_jobs = [(dt, tb) for dt in range(2) for tb in range(4)]
                  dense = [("q", tb) for tb in range(4)] + [("za", j) for j in range(len(za_jobs))]

                  def za_job(dt, tb, idx):
                      bank = 6 + idx % 2
                      toks = slice(tb * 512, (tb + 1) * 512)
                      mm(ps[bank][:], [(wh[:, 3, kt, dt * 128:(dt + 1) * 128], hxT[:, kt, toks]) for kt in range(8)],
                         reads=["wh", "hxT"], writes=[PS(bank)])
                      T.task("act", lambda e: e.activation(out=zaT[:, dt, toks], in_=ps[bank][:], func=AF.Silu),
                             reads=[PS(bank)], writes=[f"zaT{tb}"])
                  ck("p2a")
                  if False:
                      T.dma("pool", "wh", [(wh[:, s_, :, :], wh_d[h + 1, :, s_, :, :]) for s_ in range(4)],
                            writes=["wh"])

                  cur = 0
                  for c in range(NT - 1, -1, -1):
                      T.task("act", lambda e, c=c, cur=cur: e.activation(out=Tb[:, c, :, :], in_=S32[:, 1, cur, :, :],
                                                                       func=AF.Copy),
                             reads=[f"S32_1_{cur}"], writes=[f"Tb{c}"])
                      if c > 0:
                          bank = c % 2
                          mm_multi([(ps[bank][:, dt * 256:(dt + 1) * 256],
                                     [(kb[:, c, dt * 128:(dt + 1) * 128], vv[:, c, :])]) for dt in range(2)],
                                   reads=["kb", "vv"], writes=[PS(bank)])
                          T.task("dve", lambda e, bank=bank, h=h, cur=cur: e.scalar_tensor_tensor(
                              out=S32[:, 1, 1 - cur, :, :], in0=S32[:, 1, cur, :, :], scalar=gC[:, 4 + h:5 + h],
                              in1=ps[bank][:].rearrange("p (a b) -> p a b", a=2), op0=ALU.mult, op1=ALU.add),
                              reads=[PS(bank), f"S32_1_{cur}", "gC"], writes=[f"S32_1_{1 - cur}"])
                          cur = 1 - cur
                      zi = NT - 1 - c
                      if zi < len(dense):
                          if dense[zi][0] == "q":
                              qk_unit(0, dense[zi][1], 2 + 2 * (zi % 2))
                          else:
                              j = dense[zi][1]
                              za_job(za_jobs[j][0], za_jobs[j][1], j)
                          if zi == len(dense) - 1 and h + 1 < H:
                              T.dma("pool", "wh", [(wh[:, s_, :, :], wh_d[h + 1, :, s_, :, :]) for s_ in range(4)],
                                    writes=["wh"])
                          if zi == len(dense) - 1 and h + 1 == H:
                              T.dma("pool", "wh", [(whv, wB_d[0])], writes=["wh"])
                  ck("p2b")

                  def O1(c, h=h):
                      s2 = c % 2
                      ct = slice(c * 128, (c + 1) * 128)
                      cur = c % 2
                      T.task("act", lambda e: e.activation(out=Sf[s2][:], in_=S32[:, 0, cur, :, :], func=AF.Copy),
                             reads=[f"S32_0_{cur}"], writes=[f"Sf{s2}"])
                      if c < NT - 1:
                          bank = 0
                          mm_multi([(ps[bank][:, dt * 256:(dt + 1) * 256],
                                     [(kf[:, c, dt * 128:(dt + 1) * 128], vv[:, c, :])]) for dt in range(2)],
                                   reads=["kf", "vv"], writes=[PS(bank)])
                          T.task("dve", lambda e: e.scalar_tensor_tensor(
                              out=S32[:, 0, 1 - cur, :, :], in0=S32[:, 0, cur, :, :], scalar=gC[:, h:h + 1],
                              in1=ps[bank][:].rearrange("p (a b) -> p a b", a=2), op0=ALU.mult, op1=ALU.add),
                              reads=[PS(bank), f"S32_0_{cur}", "gC"], writes=[f"S32_0_{1 - cur}"])

                      def qsc(e):
                          e.tensor_tensor(out=qfb[s2][:, 0, :, :], in0=qT[:, :, ct],
                                          in1=qdec[:, h, 0, :].unsqueeze(1).broadcast_to([128, 2, 128]), op=ALU.mult)
                          return e.tensor_tensor(out=qfb[s2][:, 1, :, :], in0=qT[:, :, ct],
                                                 in1=qdec[:, h, 1, :].unsqueeze(1).broadcast_to([128, 2, 128]),
                                                 op=ALU.mult)
                      T.task("pool", qsc, reads=["qT", "dectabs"], writes=[f"qfb{s2}"])
                      sbank = 2 + c % 2
                      skey = PS(sbank)
                      sT = ps[sbank][:, 0:128]
                      mm(sT, [(kT[:, dt, ct], qT[:, dt, ct]) for dt in range(2)], reads=[f"kT{c // 4}", "qT"],
                         writes=[skey])
                      T.task("dve", lambda e: e.tensor_tensor(out=PT[s2][:], in0=sT, in1=DT[:, h, :], op=ALU.mult),
                             reads=[skey, "DT"], writes=[f"PT{s2}"])

                  def O1b(c, h=h):
                      s2 = c % 2
                      o4 = c % 4
                      obank = 4 + o4
                      okey = PS(obank)
                      oap = ps[obank][:, 0:256]
                      mm(oap, [(PT[s2][:], vv[:, c, :]),
                               (qfb[s2][:, 0, 0, :], Sf[s2][:, 0, :]), (qfb[s2][:, 0, 1, :], Sf[s2][:, 1, :]),
                               (qfb[s2][:, 1, 0, :], Tb[:, c, 0, :]), (qfb[s2][:, 1, 1, :], Tb[:, c, 1, :])],
                         reads=[f"PT{s2}", "vv", f"qfb{s2}", f"Sf{s2}", f"Tb{c}"], writes=[okey])

                  def O2(c):
                      o4 = c % 4
                      obank = 4 + o4
                      okey = PS(obank)
                      oap = ps[obank][:, 0:256]
                      T.task("dve", lambda e: e.bn_stats(gst[:, o4, :], oap), reads=[okey], writes=[f"gst{o4}"])
                      T.task("dve", lambda e: e.bn_aggr(gmv[:, o4, :], gst[:, o4, :]), reads=[f"gst{o4}"],
                             writes=[f"gmv{o4}"])
                      T.task("act", lambda e: e.activation(out=gl[:, o4:o4 + 1], in_=gmv[:, o4, 1:2], func=AF.Ln,
                                                           bias=EPS),
                             reads=[f"gmv{o4}"], writes=[f"gl{o4}"])
                      T.task("act", lambda e: e.activation(out=grs[:, o4:o4 + 1], in_=gl[:, o4:o4 + 1], func=AF.Exp,
                                                           scale=-0.5),
                             reads=[f"gl{o4}"], writes=[f"grs{o4}"])

                  def O3(c, h=h):
                      o4 = c % 4
                      obank = 4 + o4
                      okey = PS(obank)
                      oap = ps[obank][:, 0:256]
                      yslot = (c // 4) % 2
                      T.task("pool", lambda e: e.tensor_scalar(
                          out=gnm[:, o4:o4 + 1], in0=gmv[:, o4, 0:1], scalar1=grs[:, o4:o4 + 1], scalar2=-1.0,
                          op0=ALU.mult, op1=ALU.mult),
                          reads=[f"gmv{o4}", f"grs{o4}"], writes=[f"gnm{o4}"])
                      T.task("act", lambda e: e.activation(out=ynb[yslot][:, o4, :], in_=oap, func=AF.Identity,
                                                           scale=grs[:, o4:o4 + 1], bias=gnm[:, o4:o4 + 1]),
                             reads=[okey, f"grs{o4}", f"gnm{o4}"], writes=[f"ynb{yslot}"])
                      if o4 == 3:
                          c4 = c // 4
                          bank = 1

                          def ytr(pe):
                              ins = None
                              for t in range(4):
                                  for dt in range(2):
                                      ins = pe.transpose(psb(bank)[:, dt * 512 + t * 128:dt * 512 + (t + 1) * 128],
                                                         ynb[yslot][:, t, dt * 128:(dt + 1) * 128], ident[:])
                              return ins
                          T.task("pe", ytr, reads=[f"ynb{yslot}", "ident"], writes=[PS(bank)])

                          def yev(e):
                              ins = None
                              for dt in range(2):
                                  ft = h * 2 + dt
                                  ins = e.scalar_tensor_tensor(
                                      out=yaT[:, ft, c4 * 512:(c4 + 1) * 512], in0=psb(bank)[:, dt * 512:(dt + 1) * 512],
                                      scalar=colv[:, 32 + ft:33 + ft], in1=zaT[:, dt, c4 * 512:(c4 + 1) * 512],
                                      op0=ALU.mult, op1=ALU.mult)
                              return ins
                          T.task("dve", yev, reads=[PS(bank), f"zaT{c4}", "consts"], writes=["yaT"])

                  for i in range(NT + 3):
                      if i < NT:
                          O1(i)
                      if 0 <= i - 1 < NT:
                          O1b(i - 1)
                      if 0 <= i - 2 < NT:
                          O2(i - 2)
                      if 0 <= i - 3 < NT:
                          O3(i - 3)
              tap("yaT", yaT[:], [128, 8, N], ["yaT"], True)
              ck("p2")
              T.barrier()
          p02.close()

          ybT = sb(outer, "ybT", [128, 8, N], BF16)
          with contextlib.ExitStack() as p3:
              vln = sb(p3, "vln", [128, NT, D], BF16)
              wBs = [whv, sb(p3, "wBs1", [128, 8, D], BF16)]
              g32 = [sb(p3, f"g32_{i}", [128, D], F32) for i in range(2)]
              lst = sb(p3, "lst", [128, NT, 12], F32)
              lmv = sb(p3, "lmv", [128, NT, 2], F32)
              ll = sb(p3, "ll", [128, NT], F32)
              lrs = sb(p3, "lrs", [128, NT], F32)
              lnm = sb(p3, "lnm", [128, NT], F32)
              wsT = sb(p3, "wsT", [128, 8, 128], BF16)
              wsT32 = sb(p3, "wsT32", [128, 8, 128], F32)
              lnbb = sb(p3, "lnbb", [128, D], F32)
              bsb = sb(p3, "bsb", [128, 8, 128], F32)
              bias2 = sb(p3, "bias2", [128, 8, 128], F32)
              szb = [sb(p3, f"szb{i}", [128, 512], F32) for i in range(2)]
              mxb = [sb(p3, f"mxb{i}", [128, 512], F32) for i in range(2)]

              T.dma("pool", "wBs1", [(wBs[1][:], wB_d[1])], writes=["wBs1"])
              T.dma("pool", "wsT", [(wsT[:], wsT_d)], writes=["wsT"])
              T.dma("sp", "c3", [(wsT32[:], wsT_d), (lnbb[:], lnb_d.partition_broadcast(128)),
                                 (bsb[:].rearrange("p a b -> p (a b)"), bs_d.partition_broadcast(128))],
                    writes=["c3"])
              for tt in range(NT):
                  s2 = tt % 2
                  b0 = 2 * s2
                  tok = slice(tt * 128, (tt + 1) * 128)
                  mm_multi([(ps[b0 + i][:], [(hxT[:, kt, tok], wBs[0][:, kt, i * 512:(i + 1) * 512]) for kt in range(8)])
                            for i in range(2)],
                           reads=["hxT", "wh"], writes=[PS(b0), PS(b0 + 1)])

                  def gev(e, s2=s2, b0=b0):
                      e.activation(out=g32[s2][:, 0:512], in_=ps[b0][:], func=AF.Gelu_apprx_tanh)
                      return e.activation(out=g32[s2][:, 512:1024], in_=ps[b0 + 1][:], func=AF.Gelu_apprx_tanh)
                  T.task("act", gev, reads=[PS(b0), PS(b0 + 1)], writes=[f"g32_{s2}"])

                  def lstat(e, s2=s2, tt=tt):
                      e.bn_stats(lst[:, tt, 0:6], g32[s2][:, 0:512])
                      return e.bn_stats(lst[:, tt, 6:12], g32[s2][:, 512:1024])
                  T.task("dve", lstat, reads=[f"g32_{s2}"], writes=[f"lst{tt}"])
                  T.task("dve", lambda e, tt=tt: e.bn_aggr(lmv[:, tt, :], lst[:, tt, :]), reads=[f"lst{tt}"],
                         writes=["lmv"])
                  T.task("dve", lambda e, s2=s2, tt=tt: e.tensor_copy(out=vln[:, tt, :], in_=g32[s2][:]),
                         reads=[f"g32_{s2}"], writes=["vln"])
              T.task("act", lambda e: e.activation(out=ll[:], in_=lmv[:, :, 1], func=AF.Ln, bias=EPS),
                     reads=["lmv"], writes=["ll"])
              T.task("act", lambda e: e.activation(out=lrs[:], in_=ll[:], func=AF.Exp, scale=-0.5),
                     reads=["ll"], writes=["lrs"])
              T.task("dve", lambda e: e.scalar_tensor_tensor(out=lnm[:], in0=lmv[:, :, 0], scalar=-1.0, in1=lrs[:],
                                                              op0=ALU.mult, op1=ALU.mult),
                     reads=["lmv", "lrs"], writes=["lnm"])

              for t4 in range(4):
                  def vnorm(e, t4=t4):
                      ins = None
                      for tt in range(t4 * 4, t4 * 4 + 4):
                          ins = e.tensor_scalar(out=vln[:, tt, :], in0=vln[:, tt, :], scalar1=lrs[:, tt:tt + 1],
                                                scalar2=lnm[:, tt:tt + 1], op0=ALU.mult, op1=ALU.add)
                      return ins
                  T.task("dve", vnorm, reads=["vln", "lrs", "lnm"], writes=["vln"])
              T.dma("pool", "wh", [(wBs[0][:], wB_d[2])], writes=["wh"])

              n = 0
              for g in range(8):
                  for tb in range(4):
                      bank = n % 4
                      n += 1
                      toks = slice(tb * 512, (tb + 1) * 512)
                      mm(ps[bank][:], [(wBs[1][:, kt, g * 128:(g + 1) * 128], hxT[:, kt, toks]) for kt in range(8)],
                         reads=["wBs1", "hxT"], writes=[PS(bank)])
                      T.task("act", lambda e, g=g, toks=toks, bank=bank: e.activation(
                          out=ybT[:, g, toks], in_=ps[bank][:], func=AF.Gelu_apprx_tanh),
                          reads=[PS(bank)], writes=["ybT"])
              for half in range(2):
                  bank = 4 + half
                  mm_multi([(ps[bank][:, gi * 128:(gi + 1) * 128],
                             [(lnbb[:, (half * 4 + gi) * 128:(half * 4 + gi + 1) * 128], wsT32[:, half * 4 + gi, :])])
                            for gi in range(4)],
                           reads=["c3"], writes=[PS(bank)])
                  T.task("dve", lambda e, half=half, bank=bank: e.tensor_tensor(
                      out=bias2[:, half * 4:(half + 1) * 4, :], in0=ps[bank][:].rearrange("p (a b) -> p a b", a=4),
                      in1=bsb[:, half * 4:(half + 1) * 4, :], op=ALU.add),
                      reads=[PS(bank), "c3"], writes=["bias2"])
              n = 0
              for g in range(8):
                  for tb in range(4):
                      s2 = n % 2
                      n += 1
                      zbank = s2
                      mbank = 2 + s2
                      toks = slice(tb * 512, (tb + 1) * 512)
                      mm(ps[zbank][:], [(wBs[0][:, kt, g * 128:(g + 1) * 128], hxT[:, kt, toks]) for kt in range(8)],
                         reads=["wh", "hxT"], writes=[PS(zbank)])
                      mm_multi([(ps[mbank][:, cc * 128:(cc + 1) * 128],
                                 [(vln[:, tb * 4 + cc, g * 128:(g + 1) * 128], wsT[:, g, :])]) for cc in range(4)],
                               reads=["vln", "wsT"], writes=[PS(mbank)])
                      T.task("act", lambda e, s2=s2, zbank=zbank: e.activation(out=szb[s2][:], in_=ps[zbank][:],
                                                                             func=AF.Silu),
                             reads=[PS(zbank)], writes=[f"szb{s2}"])
                      T.task("dve", lambda e, s2=s2, mbank=mbank, g=g: e.scalar_tensor_tensor(
                          out=mxb[s2][:].rearrange("p (a b) -> p a b", a=4),
                          in0=ps[mbank][:].rearrange("p (a b) -> p a b", a=4), scalar=colv[:, 40 + g:41 + g],
                          in1=bias2[:, g, :].unsqueeze(1).broadcast_to([128, 4, 128]), op0=ALU.mult, op1=ALU.add),
                          reads=[PS(mbank), "bias2", "consts"], writes=[f"mxb{s2}"])

                      T.task("dve", lambda e, s2=s2: e.tensor_tensor(out=mxb[s2][:], in0=mxb[s2][:], in1=szb[s2][:],
                                                                     op=ALU.mult),
                             reads=[f"mxb{s2}", f"szb{s2}"], writes=[f"mxb{s2}"])
                      T.task("pool", lambda e, s2=s2, g=g, toks=toks: e.tensor_tensor(
                          out=ybT[:, g, toks], in0=ybT[:, g, toks], in1=mxb[s2][:], op=ALU.mult),
                          reads=[f"mxb{s2}", "ybT"], writes=["ybT"])
              tap("ybT", ybT[:], [128, 8, N], ["ybT"], True)
              ck("p3")
              T.barrier()

          with contextlib.ExitStack() as p4:
              mT = sb(p4, "mT", [128, 8, N], BF16)
              w4s = [sb(p4, f"w4s{i}", [128, 4, 8, 128], BF16) for i in range(2)]
              woutb = whv
              wst = [sb(p4, f"wst{i}", [128, D], F32) for i in range(2)]
              fgb = sb(p4, "fgb", [128, D], F32)
              sga = [sb(p4, f"sga{i}", [128, 512], F32) for i in range(2)]
              sgb = [sb(p4, f"sgb{i}", [128, 512], F32) for i in range(2)]
              t1 = sga
              t2 = sgb
              xs4 = [sb(p4, f"xs4_{i}", [128, D], F32) for i in range(4)]
              junk4 = sb(p4, "junk4", [128, D], BF16)
              ss4 = sb(p4, "ss4", [128, NT], F32)
              ls4 = sb(p4, "ls4", [128, NT], F32)
              rs4 = sb(p4, "rs4", [128, NT], F32)

              T.dma("sp", "c4", [(fgb[:], fg_d.partition_broadcast(128))], writes=["fgb"])
              for d0 in range(2):
                  T.dma("pool", f"w4s{d0}", [(w4s[d0][:], w4_d[d0])], writes=[f"w4s{d0}"])
              T.task("pool", lambda e: e.memset(ss4[:], 0.0), writes=["ss4"])
              n = 0
              for dtile in range(8):
                  ws2 = dtile % 2
                  for tb in range(4):
                      s2 = n % 2
                      n += 1
                      base = 4 * s2
                      toks = slice(tb * 512, (tb + 1) * 512)
                      srcs = (hxT, hxT, yaT, ybT)
                      keys = ("hxT", "hxT", "yaT", "ybT")
                      for m in range(4):
                          mm(ps[base + m][:], [(w4s[ws2][:, m, kt, :], srcs[m][:, kt, toks]) for kt in range(8)],
                             reads=[f"w4s{ws2}", keys[m]], writes=[PS(base + m)])

                      def sig(e, s2=s2, base=base):
                          e.activation(out=sga[s2][:], in_=ps[base][:], func=AF.Sigmoid)
                          return e.activation(out=sgb[s2][:], in_=ps[base + 1][:], func=AF.Sigmoid)
                      T.task("act", sig, reads=[PS(base), PS(base + 1)], writes=[f"sg{s2}"])

                      def mrg(e, s2=s2, base=base):
                          e.tensor_tensor(out=t1[s2][:], in0=ps[base + 2][:], in1=sga[s2][:], op=ALU.mult)
                          return e.tensor_tensor(out=t2[s2][:], in0=ps[base + 3][:], in1=sgb[s2][:], op=ALU.mult)
                      T.task("dve", mrg, reads=[PS(base + 2), PS(base + 3), f"sg{s2}"], writes=[f"sg{s2}"])
                      T.task("pool", lambda e, s2=s2, dtile=dtile, toks=toks: e.tensor_tensor(
                          out=mT[:, dtile, toks], in0=t1[s2][:], in1=t2[s2][:], op=ALU.add),
                          reads=[f"sg{s2}"], writes=["mT"])
                  if dtile + 2 < 8:
                      T.dma("pool", f"w4s{ws2}", [(w4s[ws2][:], w4_d[dtile + 2])], writes=[f"w4s{ws2}"])
                  pc = dtile
                  T.dma("sp", f"wst{pc % 2}", [(wst[pc % 2][:], wout_d[:, pc, :])], writes=[f"wst{pc % 2}"])
                  T.task("dve", lambda e, pc=pc: e.tensor_tensor(
                      out=woutb[:, pc, :], in0=wst[pc % 2][:], in1=gtb[:], op=ALU.mult),
                      reads=[f"wst{pc % 2}", "gtb"], writes=["wh"])
              tap("mT", mT[:], [128, 8, N], ["mT"], True)
              ck("p4a")
              def F1(tt):
                  s3 = tt % 4
                  b0 = 2 * (tt % 4)
                  tok = slice(tt * 128, (tt + 1) * 128)
                  if tt == 0:
                      for t0 in range(2):
                          T.dma("sp", f"xs4_{t0}", [(xs4[t0][:], x_d[t0 * 128:(t0 + 1) * 128, :])],
                                writes=[f"xs4_{t0}"])
                  if tt + 2 < NT:
                      sn = (tt + 2) % 4
                      T.dma("sp", f"xs4_{sn}", [(xs4[sn][:], x_d[(tt + 2) * 128:(tt + 3) * 128, :])],
                            writes=[f"xs4_{sn}"])
                  mm_multi([(ps[b0 + i][:], [(mT[:, kt, tok], woutb[:, kt, i * 512:(i + 1) * 512]) for kt in range(8)])
                            for i in range(2)],
                           reads=["mT", "wh"], writes=[PS(b0), PS(b0 + 1)])

                  def resid(e):
                      e.tensor_tensor(out=xs4[s3][:, 0:512], in0=ps[b0][:], in1=xs4[s3][:, 0:512], op=ALU.add)
                      return e.tensor_tensor(out=xs4[s3][:, 512:1024], in0=ps[b0 + 1][:], in1=xs4[s3][:, 512:1024],
                                             op=ALU.add)
                  T.task("dve", resid, reads=[PS(b0), PS(b0 + 1), f"xs4_{s3}"], writes=[f"xs4_{s3}"])
                  T.task("act", lambda e: e.activation(out=junk4[:], in_=xs4[s3][:], func=AF.Square,
                                                       accum_out=ss4[:, tt:tt + 1]),
                         reads=[f"xs4_{s3}", "ss4"], writes=["junk4", f"ss4_{tt}"])
                  T.task("act", lambda e: e.activation(out=ls4[:, tt:tt + 1], in_=ss4[:, tt:tt + 1], func=AF.Ln,
                                                       scale=1.0 / D, bias=EPS),
                         reads=[f"ss4_{tt}"], writes=[f"ls4_{tt}"])
                  T.task("act", lambda e: e.activation(out=rs4[:, tt:tt + 1], in_=ls4[:, tt:tt + 1],
                                                       func=AF.Exp, scale=-0.5),
                         reads=[f"ls4_{tt}"], writes=[f"rs4_{tt}"])

              def F2(tt):
                  s3 = tt % 4
                  tok = slice(tt * 128, (tt + 1) * 128)
                  T.task("dve", lambda e: e.scalar_tensor_tensor(
                      out=xs4[s3][:], in0=xs4[s3][:], scalar=rs4[:, tt:tt + 1], in1=fgb[:], op0=ALU.mult,
                      op1=ALU.mult),
                      reads=[f"xs4_{s3}", f"rs4_{tt}", "fgb"], writes=[f"xs4_{s3}"])
                  T.dma("sp", f"xs4_{s3}", [(y_d[tok, :], xs4[s3][:])], reads=[f"xs4_{s3}"])

              for i in range(NT + 1):
                  if i < NT:
                      F1(i)
                  if i >= 1:
                      F2(i - 1)
              T.finish("sp")
    except _Stop:
        T.finish("sp")
    return nc, tap_out, T


def _kt_layout(w):
    k, c = w.shape
    return np.ascontiguousarray(w.reshape(8, 128, c).transpose(1, 0, 2))


def _col_layout(v):
    return np.ascontiguousarray(v.reshape(-1, 128).T)


def _host_consts():
    f32 = np.float32
    n = np.arange(N)
    r = (n // 64).astype(np.float64)
    col = (n % 64).astype(np.float64)
    quarter = 64
    inv = np.power(10000.0, -np.arange(quarter, dtype=np.float64) / quarter)
    ang = np.concatenate([r[None, :] * inv[:, None], col[None, :] * inv[:, None]], axis=0)
    rope = np.stack([np.cos(ang), np.sin(ang)], axis=1).astype(f32)
    j = np.arange(128, dtype=f32)[:, None]
    i = np.arange(128, dtype=f32)[None, :]
    diffF = np.maximum(i - j, 0)
    diffB = np.maximum(j - i, 0)
    maskF = (i >= j).astype(f32) / 16.0
    maskB = (j > i).astype(f32) / 16.0
    cmat = np.stack([diffF, diffB, maskF, maskB], axis=1).astype(f32)
    return rope, cmat


def _prepare(inputs):
    f32 = np.float32
    g = {k: np.asarray(v, dtype=f32) for k, v in inputs.items()}
    w_in = g["w_in"][0]
    rope, cmat = _host_consts()
    shared = {}
    shared["wmod"] = _kt_layout(g["w_mod"][0])
    colv = np.zeros((128, 64), f32)
    colv[:, 0:24] = _col_layout(g["b_mod"][0])
    colv[:, 24:32] = _col_layout(g["norm_g"][0])
    colv[:, 32:40] = _col_layout(g["ret_norm_g"][0])
    colv[:, 40:48] = _col_layout(g["mlp_ln_g"][0])
    p = np.arange(128, dtype=f32)
    colv[:, 48] = 127.0 - p
    colv[:, 49] = p
    colv[:, 50] = 255.0 - p
    colv[:, 51] = 255.0 - (128.0 + p)
    colv[:, 52] = p
    colv[:, 53] = 128.0 + p
    shared["colv"] = colv
    rowA = np.zeros((1, 264), f32)
    rowA[0, 0:4] = g["ret_decay_fwd"][0]
    rowA[0, 4:8] = g["ret_decay_bwd"][0]
    rowA[0, 8:136] = np.arange(128, dtype=f32) + 1.0
    rowA[0, 136:264] = 128.0 - np.arange(128, dtype=f32)
    shared["rowA"] = rowA
    shared["bmod2"] = np.ascontiguousarray(np.stack([g["b_mod"][0], g["b_mod"][0]], axis=0))
    shared["fg"] = np.ascontiguousarray(g["final_norm_g"][None, :])
    shared["lnb"] = np.ascontiguousarray(g["mlp_ln_b"][0][None, :])
    shared["bs"] = np.ascontiguousarray(g["mlp_bs"][0].reshape(1, 8 * 128))
    shared["cmat"] = cmat
    shared["rope"] = rope
    shared["ident"] = np.eye(128, dtype=f32)
    shared["wsT"] = np.ascontiguousarray(g["mlp_ws"][0].transpose(2, 0, 1))
    wh = np.zeros((H, 128, 4, 8, 256), f32)
    for h in range(H):
        for s in range(4):
            c0 = s * 1024 + h * 256
            wh[h, :, s] = _kt_layout(w_in[:, c0:c0 + 256])
    shared["wh"] = wh
    shared["wB"] = np.stack([_kt_layout(w_in[:, 5 * 1024:6 * 1024]), _kt_layout(w_in[:, 4 * 1024:5 * 1024]),
                             _kt_layout(w_in[:, 6 * 1024:7 * 1024])], axis=0)
    w4 = np.zeros((8, 128, 4, 8, 128), f32)
    mats = (w_in[:, 7 * 1024:8 * 1024], w_in[:, 8 * 1024:9 * 1024], g["w_proj_a"][0], g["w_proj_b"][0])
    for dt in range(8):
        for m in range(4):
            w4[dt, :, m] = _kt_layout(mats[m][:, dt * 128:(dt + 1) * 128])
    shared["w4"] = w4
    shared["wout"] = _kt_layout(g["w_out"][0])
    in_maps = []
    for b in range(8):
        d = dict(shared)
        d["x"] = np.ascontiguousarray(g["x"][b])
        d["ctx"] = np.ascontiguousarray(g["ctx"][b])
        cc = np.stack([g["c"][b], g["c_ctx"]], axis=-1)
        d["cT"] = np.ascontiguousarray(cc.reshape(8, 128, 2).transpose(1, 0, 2))
        in_maps.append(d)
    return in_maps


def kernel(**inputs):
    in_maps = _prepare(inputs)
    nc, _, _ = build()
    res = run_bass_kernel_spmd(nc, in_maps, core_ids=list(range(8)))
    out = np.stack([np.asarray(res.results[b]["y"], dtype=np.float32) for b in range(8)], axis=0)
    return out
```

```python
import contextlib
import numpy as np
import concourse.bass as bass
import concourse.mybir as mybir
from concourse.bass_utils import run_bass_kernel_spmd

F32 = mybir.dt.float32
BF16 = mybir.dt.bfloat16
AF = mybir.ActivationFunctionType
ALU = mybir.AluOpType

D = 1024
N = 2048
NCTX = 256
H = 4
C = 128
EPS = 1e-6
NT = N // 128
LN16 = float(np.log(1.0 / 16.0))


class _Stop(Exception):
    pass


ATTACH_WAITS = True


class _Rec:
    def __init__(self, e):
        self._e = e
        self.first = None

    def __getattr__(self, name):
        attr = getattr(self._e, name)
        if not callable(attr):
            return attr

        def w(*a, **k):
            r = attr(*a, **k)
            if self.first is None and r is not None:
                self.first = r
            return r
        return w


class Trk:
    def __init__(self, nc):
        self.nc = nc
        self.E = {}
        for name, e in (("pe", nc.tensor), ("act", nc.scalar), ("dve", nc.vector),
                        ("pool", nc.gpsimd), ("sp", nc.sync)):
            self.E[name] = dict(e=e, sem=nc.alloc_semaphore("s_" + name), cnt=0, known={}, mult=1)
        self.res = {}
        self.snap = {}
        self.nwaits = 0
        self.ntasks = 0
        self.limit = 0

    def dsem(self, name):
        pn = "d:" + name
        if pn not in self.E:
            self.E[pn] = dict(e=None, sem=self.nc.alloc_semaphore("sd_" + name), cnt=0, known={}, mult=16)
        return pn

    def _deps(self, en, reads, writes, strict=False):
        need = {}

        def add(dep, raw):
            n, s = dep
            if n == en and not raw and not strict:
                return
            if need.get(n, 0) < s:
                need[n] = s

        for k in reads:
            r = self.res.get(k)
            if r and r[0]:
                add(r[0], True)
        for k in writes:
            r = self.res.get(k)
            if r:
                if r[0]:
                    add(r[0], False)
                for d in r[1].items():
                    add(d, False)
        return need

    def _wait(self, en, need, defer_last=False):
        E = self.E[en]
        todo = []
        for n, s in sorted(need.items()):
            if E["known"].get(n, 0) >= s:
                continue
            src = self.E[n]
            todo.append((src["sem"], s * src["mult"]))
            self.nwaits += 1
            E["known"][n] = s
            sn = self.snap.get((n, s))
            if sn:
                for k2, v2 in sn.items():
                    if k2 != en and E["known"].get(k2, 0) < v2:
                        E["known"][k2] = v2
        last = None
        if defer_last and todo:
            last = todo.pop()
        for sem, val in todo:
            E["e"].wait_ge(sem, val)
        return last

    def _commit(self, prod, seq, reads, writes):
        for k in writes:
            self.res[k] = [(prod, seq), {}]
        for k in reads:
            r = self.res.setdefault(k, [None, {}])
            if r[1].get(prod, 0) < seq:
                r[1][prod] = seq

    def task(self, en, fn, reads=(), writes=()):
        need = self._deps(en, reads, writes, strict=(en == "pool"))
        pend = self._wait(en, need, defer_last=ATTACH_WAITS)
        E = self.E[en]
        if pend is not None:
            rec = _Rec(E["e"])
            last = fn(rec)
            rec.first._wait_ge(pend[0], pend[1])
        else:
            last = fn(E["e"])
        E["cnt"] += 1
        last.then_inc(E["sem"], 1)
        self.snap[(en, E["cnt"])] = dict(E["known"])
        self._commit(en, E["cnt"], reads, writes)
        self.ntasks += 1
        if self.limit and self.ntasks >= self.limit:
            raise _Stop()

    def dma(self, qn, dname, pairs, reads=(), writes=()):
        pn = self.dsem(dname)
        need = self._deps(qn, reads, writes, strict=True)
        P = self.E[pn]
        if P["cnt"] > 0:
            if need.get(pn, 0) < P["cnt"]:
                need[pn] = P["cnt"]
        self._wait(qn, need)
        Q = self.E[qn]
        for (o, i) in pairs:
            Q["e"].dma_start(out=o, in_=i).then_inc(P["sem"], 16)
            P["cnt"] += 1
        self.snap[(pn, P["cnt"])] = dict(Q["known"])
        self._commit(pn, P["cnt"], reads, writes)

    def barrier(self):
        names = [n for n in self.E]
        for en in ("pe", "act", "dve", "pool", "sp"):
            need = {n: self.E[n]["cnt"] for n in names if self.E[n]["cnt"] > 0}
            self._wait(en, need)

    def finish(self, en="sp"):
        need = {n: self.E[n]["cnt"] for n in self.E if n != en and self.E[n]["cnt"] > 0}
        self._wait(en, need)


def build(taps=None, stop=None):
    taps = taps or set()

    def ck(name):
        if stop == name:
            raise _Stop()
    nc = bass.Bass("TRN2", target_bir_lowering=False)
    T = Trk(nc)
    if isinstance(stop, int):
        T.limit = stop

    def dram_in(name, shape):
        return nc.dram_tensor(name, list(shape), F32, kind="ExternalInput").ap()

    x_d = dram_in("x", [N, D])
    ctx_d = dram_in("ctx", [NCTX, D])
    cT_d = dram_in("cT", [128, 8, 2])
    wmod_d = dram_in("wmod", [128, 8, 3 * D])
    colv_d = dram_in("colv", [128, 64])
    rowA_d = dram_in("rowA", [1, 264])
    bmod2_d = dram_in("bmod2", [2, 3 * D])
    fg_d = dram_in("fg", [1, D])
    lnb_d = dram_in("lnb", [1, D])
    bs_d = dram_in("bs", [1, 8 * 128])
    cmat_d = dram_in("cmat", [128, 4, 128])
    rope_d = dram_in("rope", [128, 2, N])
    ident_d = dram_in("ident", [128, 128])
    wsT_d = dram_in("wsT", [128, 8, 128])
    wh_d = dram_in("wh", [H, 128, 4, 8, 256])
    wB_d = dram_in("wB", [3, 128, 8, D])
    w4_d = dram_in("w4", [8, 128, 4, 8, 128])
    wout_d = dram_in("wout", [128, 8, D])
    y_d = nc.dram_tensor("y", [N, D], F32, kind="ExternalOutput").ap()

    tap_out = {}

    ps = [nc.alloc_psum_tensor(f"ps{i}", [128, 512], F32) for i in range(8)]

    def PS(i):
        return f"ps{i}"

    def psb(i):
        return ps[i][:].bitcast(BF16)

    outer = contextlib.ExitStack()

    def sb(stack, name, shape, dt):
        return stack.enter_context(nc.sbuf_tensor("sb_" + name, list(shape), dt))

    def tap(name, ap, shape, reads, is_bf16):
        if name not in taps:
            return
        d = nc.dram_tensor("tap_" + name, list(shape), F32, kind="ExternalOutput").ap()
        tap_out[name] = d
        if is_bf16:
            T.dma("pool", "tap_" + name, [(d, ap)], reads=reads)
        else:
            T.dma("sp", "tap_" + name, [(d, ap)], reads=reads)

    def mm(out_ap, pairs, reads, writes, extra=None):
        def fn(pe):
            n = len(pairs)
            ins = None
            for i, (l, r) in enumerate(pairs):
                ins = pe.matmul(out_ap, l, r, start=(i == 0), stop=(i == n - 1))
            return ins
        T.task("pe", fn, reads, writes)

    def mm_multi(groups, reads, writes):
        def fn(pe):
            ins = None
            for out_ap, pairs in groups:
                n = len(pairs)
                for i, (l, r) in enumerate(pairs):
                    ins = pe.matmul(out_ap, l, r, start=(i == 0), stop=(i == n - 1))
            return ins
        T.task("pe", fn, reads, writes)

    try:
      with outer:
          hxT = sb(outer, "hxT", [128, 8, N], BF16)
          yaT = sb(outer, "yaT", [128, 8, N], BF16)
          colv = sb(outer, "colv", [128, 64], F32)
          mcol = sb(outer, "mcol", [128, 16, 2], F32)
          acol = sb(outer, "acol", [128, 2, 8], F32)
          gtb = sb(outer, "gtb", [128, D], F32)
          lg = sb(outer, "lg", [128, 8], F32)
          gC = sb(outer, "gC", [128, 8], F32)
          wh = sb(outer, "wh", [128, 4, 8, 256], BF16)
          whv = wh[:].rearrange("p a b c -> p (a b c)").rearrange("p (k n) -> p k n", k=8)
          p02 = contextlib.ExitStack()
          hcT = sb(p02, "hcT", [128, 8, NCTX], BF16)
          ident = sb(p02, "ident", [128, 128], BF16)
          rowA = sb(p02, "rowA", [128, 264], F32)
          cmat = sb(p02, "cmat", [128, 4, 128], F32)
          DT = sb(p02, "DT", [128, H, 128], F32)
          qdec = sb(p02, "qdec", [128, H, 2, 128], F32)
          kdec = sb(p02, "kdec", [128, 8], F32)
          cdec = sb(p02, "cdec", [128, 2, 8], F32)
          rope = sb(p02, "rope", [128, 2, N], F32)

          T.dma("sp", "c", [(colv[:], colv_d), (rowA[:], rowA_d.partition_broadcast(128)),
                            (cmat[:], cmat_d)], writes=["consts"])
          T.dma("pool", "c2", [(ident[:], ident_d)], writes=["ident"])

          with contextlib.ExitStack() as p0:
              cT = sb(p0, "cT", [128, 8, 2], F32)
              sc = sb(p0, "sc", [128, 8, 2], BF16)
              wm = [sb(p0, f"wm{i}", [128, 8, 512], BF16) for i in range(3)]
              bmod2 = sb(p0, "bmod2", [2, 3 * D], F32)
              mrow = sb(p0, "mrow", [2, 3 * D], F32)
              id2 = sb(p0, "id2", [2, 2], F32)
              ones = sb(p0, "ones", [1, 128], F32)
              tmp8 = sb(p0, "tmp8", [128, 2, 8], F32)
              tmpE = sb(p0, "tmpE", [128, H, 2, 128], F32)

              T.dma("sp", "c", [(cT[:], cT_d), (bmod2[:], bmod2_d), (id2[:], ident_d[0:2, 0:2])],
                    writes=["consts2"])
              T.task("pool", lambda e: e.memset(ones[:], 1.0), writes=["ones"])
              T.task("act", lambda e: e.activation(out=sc[:], in_=cT[:], func=AF.Silu),
                     reads=["consts2"], writes=["sc"])

              def mod_dma(blk):
                  sl = blk % 3
                  T.dma("pool", f"wm{sl}", [(wm[sl][:], wmod_d[:, :, blk * 512:(blk + 1) * 512])], writes=[f"wm{sl}"])

              def mod_block(blk):
                  sl = blk % 3
                  bank = 4 + blk % 4
                  mm(ps[bank][0:2, :], [(sc[:, kt, :], wm[sl][:, kt, :]) for kt in range(8)],
                     reads=[f"wm{sl}", "sc"], writes=[PS(bank)])
                  T.task("dve", lambda e: e.tensor_tensor(
                      out=mrow[0:2, blk * 512:(blk + 1) * 512], in0=ps[bank][0:2, :],
                      in1=bmod2[0:2, blk * 512:(blk + 1) * 512], op=ALU.add),
                      reads=[PS(bank), "consts2"], writes=["mrow"])
                  if blk + 3 < 6:
                      mod_dma(blk + 3)
                  if blk == 2:
                      T.dma("pool", "wh", [(wh[:, s_, :, :], wh_d[0, :, s_, :, :]) for s_ in range(4)], writes=["wh"])
              mod_dma(0)
              mod_dma(1)
              mod_dma(2)

              xs = [sb(p0, f"xs{i}", [128, D], F32) for i in range(4)]
              xb = [sb(p0, f"xb{i}", [128, D], BF16) for i in range(2)]
              junk = sb(p0, "junk", [128, D], BF16)
              ss = sb(p0, "ss", [128, 20], F32)
              lss = sb(p0, "lss", [128, 20], F32)
              rstd = sb(p0, "rstd", [128, 20], F32)
              T.task("pool", lambda e: e.memset(ss[:], 0.0), writes=["ss"])
              ntile = NT + NCTX // 128
              for tt in range(ntile):
                  s3 = tt % 4
                  s2 = tt % 2
                  src = x_d[tt * 128:(tt + 1) * 128, :] if tt < NT else ctx_d[(tt - NT) * 128:(tt - NT + 1) * 128, :]
                  T.dma("sp", f"xs{s3}", [(xs[s3][:], src)], writes=[f"xs{s3}"])
                  T.task("act", lambda e, s3=s3, tt=tt: e.activation(out=junk[:], in_=xs[s3][:], func=AF.Square,
                                                                       accum_out=ss[:, tt:tt + 1]),
                         reads=[f"xs{s3}", "ss"], writes=["junk", f"ss{tt}"])
                  T.task("act", lambda e, tt=tt: e.activation(out=lss[:, tt:tt + 1], in_=ss[:, tt:tt + 1], func=AF.Ln,
                                                                scale=1.0 / D, bias=EPS),
                         reads=[f"ss{tt}"], writes=[f"lss{tt}"])
                  T.task("act", lambda e, tt=tt: e.activation(out=rstd[:, tt:tt + 1], in_=lss[:, tt:tt + 1],
                                                                func=AF.Exp, scale=-0.5),
                         reads=[f"lss{tt}"], writes=[f"rstd{tt}"])
                  T.task("dve", lambda e, s3=s3, s2=s2, tt=tt: e.tensor_scalar(
                      out=xb[s2][:], in0=xs[s3][:], scalar1=rstd[:, tt:tt + 1], scalar2=None, op0=ALU.mult),
                      reads=[f"xs{s3}", f"rstd{tt}"], writes=[f"xb{s2}"])
                  bank = tt % 4

                  def tr(pe, s2=s2, bank=bank):
                      ins = None
                      for kt in range(8):
                          ins = pe.transpose(psb(bank)[:, kt * 128:(kt + 1) * 128], xb[s2][:, kt * 128:(kt + 1) * 128],
                                             ident[:])
                      return ins
                  T.task("pe", tr, reads=[f"xb{s2}", "ident"], writes=[PS(bank)])
                  if tt < NT:
                      dst = hxT[:, :, tt * 128:(tt + 1) * 128]
                      key = "hxT"
                  else:
                      dst = hcT[:, :, (tt - NT) * 128:(tt - NT + 1) * 128]
                      key = "hcT"
                  T.task("dve", lambda e, dst=dst, bank=bank: e.tensor_copy(
                      out=dst, in_=psb(bank).rearrange("p (k t) -> p k t", k=8)),
                      reads=[PS(bank)], writes=[key])
                  if tt % 3 == 2:
                      mod_block(tt // 3)

              def mtr(pe):
                  ins = None
                  for t in range(16):
                      ins = pe.transpose(ps[6][:, 2 * t:2 * t + 2], mrow[0:2, t * 128:(t + 1) * 128], id2[0:2, 0:2])
                  return ins
              T.task("pe", mtr, reads=["mrow", "consts2"], writes=[PS(6)])
              T.task("dve", lambda e: e.tensor_copy(out=mcol[:], in_=ps[6][:, 0:32].rearrange("p (t w) -> p t w", w=2)),
                     reads=[PS(6)], writes=["mcol"])

              def acol_build(e):
                  ins = None
                  for w in range(2):
                      ins = e.scalar_tensor_tensor(out=acol[:, w, :], in0=mcol[:, 8:16, w], scalar=1.0,
                                                   in1=colv[:, 24:32], op0=ALU.add, op1=ALU.mult)
                  return ins
              T.task("dve", acol_build, reads=["mcol", "consts"], writes=["acol"])
              for half in range(2):
                  mm(ps[7][:], [(ones[0:1, :], mrow[0:1, 2 * D + half * 512:2 * D + (half + 1) * 512])],
                     reads=["ones", "mrow"], writes=[PS(7)])
                  T.task("dve", lambda e, half=half: e.tensor_copy(out=gtb[:, half * 512:(half + 1) * 512],
                                                                   in_=ps[7][:]),
                         reads=[PS(7)], writes=["gtb"])
              tap("gtb", gtb[:], [128, D], ["gtb"], False)

              T.task("act", lambda e: e.activation(out=tmp8[:, 0, :], in_=rowA[:, 0:8], func=AF.Exp, scale=-1.0),
                     reads=["consts"], writes=["tmp8a"])
              T.task("act", lambda e: e.activation(out=tmp8[:, 1, :], in_=tmp8[:, 0, :], func=AF.Ln, bias=1.0),
                     reads=["tmp8a"], writes=["tmp8b"])
              T.task("dve", lambda e: e.tensor_scalar(out=lg[:], in0=tmp8[:, 1, :], scalar1=-1.0, scalar2=None,
                                                       op0=ALU.mult),
                     reads=["tmp8b"], writes=["lg"])
              T.task("act", lambda e: e.activation(out=gC[:], in_=lg[:], func=AF.Exp, scale=float(C)),
                     reads=["lg"], writes=["gC"])

              def dec_tables(e):
                  ins = None
                  for h in range(H):
                      lf = lg[:, h:h + 1]
                      lb = lg[:, 4 + h:5 + h]
                      e.activation(out=tmpE[:, h, 0, :], in_=cmat[:, 0, :], func=AF.Exp, scale=lf)
                      e.activation(out=tmpE[:, h, 1, :], in_=cmat[:, 1, :], func=AF.Exp, scale=lb)
                      e.activation(out=qdec[:, h, 0, :], in_=rowA[:, 8:136], func=AF.Exp, scale=lf, bias=LN16)
                      e.activation(out=qdec[:, h, 1, :], in_=rowA[:, 136:264], func=AF.Exp, scale=lb, bias=LN16)
                      e.activation(out=kdec[:, h:h + 1], in_=colv[:, 48:49], func=AF.Exp, scale=lf)
                      e.activation(out=kdec[:, 4 + h:5 + h], in_=colv[:, 49:50], func=AF.Exp, scale=lb)
                      for j in range(2):
                          e.activation(out=cdec[:, j, h:h + 1], in_=colv[:, 50 + j:51 + j], func=AF.Exp, scale=lf)
                          ins = e.activation(out=cdec[:, j, 4 + h:5 + h], in_=colv[:, 52 + j:53 + j],
                                             func=AF.Exp, scale=lb)
                  return ins
              T.task("act", dec_tables, reads=["lg", "consts"], writes=["tmpE", "dectabs"])

              def dt_build(e):
                  ins = None
                  for h in range(H):
                      e.tensor_tensor(out=tmpE[:, h, 0, :], in0=tmpE[:, h, 0, :], in1=cmat[:, 2, :], op=ALU.mult)
                      ins = e.tensor_tensor(out=tmpE[:, h, 1, :], in0=tmpE[:, h, 1, :], in1=cmat[:, 3, :],
                                            op=ALU.mult)
                  return ins
              T.task("dve", dt_build, reads=["tmpE", "consts"], writes=["tmpE2"])
              T.task("dve", lambda e: e.tensor_tensor(out=DT[:], in0=tmpE[:, :, 0, :], in1=tmpE[:, :, 1, :],
                                                       op=ALU.add),
                     reads=["tmpE2"], writes=["DT"])

              def affine_x(e):
                  ins = None
                  for kt in range(8):
                      ins = e.tensor_scalar(out=hxT[:, kt, :], in0=hxT[:, kt, :], scalar1=acol[:, 0, kt:kt + 1],
                                            scalar2=mcol[:, kt, 0:1], op0=ALU.mult, op1=ALU.add)
                  return ins

              def affine_c(e):
                  ins = None
                  for kt in range(8):
                      ins = e.tensor_scalar(out=hcT[:, kt, :], in0=hcT[:, kt, :], scalar1=acol[:, 1, kt:kt + 1],
                                            scalar2=mcol[:, kt, 1:2], op0=ALU.mult, op1=ALU.add)
                  return ins
              T.task("dve", affine_c, reads=["acol", "mcol", "hcT"], writes=["hcT"])
              T.task("dve", affine_x, reads=["acol", "mcol", "hxT"], writes=["hxT"])
              tap("hxT", hxT[:], [128, 8, N], ["hxT"], True)
              tap("DT", DT[:], [128, H, 128], ["DT"], False)
              T.dma("sp", "rope", [(rope[:], rope_d)], writes=["rope"])
              ck("p1")
              T.barrier()

          with contextlib.ExitStack() as p2:
              qT = sb(p2, "qT", [128, 2, N], BF16)
              kT = sb(p2, "kT", [128, 2, N], BF16)
              kf = sb(p2, "kf", [128, NT, 256], BF16)
              kb = sb(p2, "kb", [128, NT, 256], BF16)
              vv = sb(p2, "vv", [128, NT, 256], BF16)
              zaT = sb(p2, "zaT", [128, 2, N], BF16)
              rtmp = [sb(p2, f"rtmp{i}", [128, 4, 256], F32) for i in range(2)]
              Tb = sb(p2, "Tb", [128, NT, 2, 256], BF16)
              S32 = sb(p2, "S32", [128, 2, 2, 2, 256], F32)
              Sf = [sb(p2, f"Sf{i}", [128, 2, 256], BF16) for i in range(2)]
              qfb = [sb(p2, f"qfb{i}", [128, 2, 2, 128], BF16) for i in range(2)]
              PT = [sb(p2, f"PT{i}", [128, 128], BF16) for i in range(2)]
              ynb = [sb(p2, f"ynb{i}", [128, 4, 256], BF16) for i in range(2)]
              kcf = sb(p2, "kcf", [128, 2, 256], BF16)
              kcb = sb(p2, "kcb", [128, 2, 256], BF16)
              vc = sb(p2, "vc", [128, 2, 256], BF16)
              gst = sb(p2, "gst", [128, 4, 6], F32)
              gmv = sb(p2, "gmv", [128, 4, 2], F32)
              gl = sb(p2, "gl", [128, 4], F32)
              grs = sb(p2, "grs", [128, 4], F32)
              gnm = sb(p2, "gnm", [128, 4], F32)

              for h in range(H):
                  for j in range(2):
                      bank = 4 + j
                      tok = slice(j * 128, (j + 1) * 128)
                      mm_multi([(ps[bank][:, 0:256], [(hcT[:, kt, tok], wh[:, 1, kt, :]) for kt in range(8)]),
                                (ps[bank][:, 256:512], [(hcT[:, kt, tok], wh[:, 2, kt, :]) for kt in range(8)])],
                               reads=["hcT", "wh"], writes=[PS(bank)])
                      ck("a0b")

                      def cev(e, j=j, bank=bank, h=h):
                          e.activation(out=kcf[:, j, :], in_=ps[bank][:, 0:256], func=AF.Copy,
                                       scale=cdec[:, j, h:h + 1])
                          return e.activation(out=kcb[:, j, :], in_=ps[bank][:, 0:256], func=AF.Copy,
                                              scale=cdec[:, j, 4 + h:5 + h])
                      T.task("act", cev, reads=[PS(bank), "dectabs"], writes=["kc"])
                      ck("a0c")
                      T.task("act", lambda e, j=j, bank=bank: e.activation(out=vc[:, j, :], in_=ps[bank][:, 256:512],
                                                                             func=AF.Copy),
                             reads=[PS(bank)], writes=["vc"])
                  for dr, kc in ((0, kcf), (1, kcb)):
                      bank = 6 + dr
                      mm_multi([(ps[bank][:, dt * 256:(dt + 1) * 256],
                                 [(kc[:, j, dt * 128:(dt + 1) * 128], vc[:, j, :]) for j in range(2)])
                                for dt in range(2)],
                               reads=["kc", "vc"], writes=[PS(bank)])
                      T.task("dve", lambda e, dr=dr, bank=bank: e.tensor_copy(
                          out=S32[:, dr, 0, :, :], in_=ps[bank][:].rearrange("p (a b) -> p a b", a=2)),
                          reads=[PS(bank)], writes=[f"S32_{dr}_0"])
                  ck("a1")

                  def qk_unit(s, tb, ba, h=h):
                      dst, dkey = ((qT, "qT"), (kT, f"kT{tb}"))[s]
                      toks = slice(tb * 512, (tb + 1) * 512)
                      bb = ba + 1
                      mm(ps[ba][:], [(wh[:, s, kt, 0:128], hxT[:, kt, toks]) for kt in range(8)],
                         reads=["wh", "hxT"], writes=[PS(ba)])
                      mm(ps[bb][:], [(wh[:, s, kt, 128:256], hxT[:, kt, toks]) for kt in range(8)],
                         reads=["wh", "hxT"], writes=[PS(bb)])
                      for hf in range(2):
                          rt = rtmp[hf]
                          lo = tb * 512 + hf * 256
                          cs = rope[:, 0, lo:lo + 256]
                          sn = rope[:, 1, lo:lo + 256]
                          pa = ps[ba][:, hf * 256:(hf + 1) * 256]
                          pb = ps[bb][:, hf * 256:(hf + 1) * 256]

                          def rmul(e, rt=rt, cs=cs, sn=sn, pa=pa, pb=pb):
                              e.tensor_tensor(out=rt[:, 0, :], in0=pa, in1=cs, op=ALU.mult)
                              e.tensor_tensor(out=rt[:, 1, :], in0=pb, in1=sn, op=ALU.mult)
                              e.tensor_tensor(out=rt[:, 2, :], in0=pa, in1=sn, op=ALU.mult)
                              return e.tensor_tensor(out=rt[:, 3, :], in0=pb, in1=cs, op=ALU.mult)
                          T.task("dve", rmul, reads=[PS(ba), PS(bb), "rope"], writes=[f"rtmp{hf}"])

                          def radd(e, rt=rt, dst=dst, lo=lo):
                              e.tensor_tensor(out=dst[:, 0, lo:lo + 256], in0=rt[:, 0, :], in1=rt[:, 1, :],
                                              op=ALU.subtract)
                              return e.tensor_tensor(out=dst[:, 1, lo:lo + 256], in0=rt[:, 2, :], in1=rt[:, 3, :],
                                                     op=ALU.add)
                          T.task("pool", radd, reads=[f"rtmp{hf}"], writes=[dkey])

                  def ktr_group(g4, h=h):
                      bank = 4 + g4 % 2

                      def ktr(pe):
                          ins = None
                          for t in range(4):
                              tt = g4 * 4 + t
                              for dt in range(2):
                                  ins = pe.transpose(psb(bank)[:, t * 256 + dt * 128:t * 256 + (dt + 1) * 128],
                                                     kT[:, dt, tt * 128:(tt + 1) * 128], ident[:])
                          return ins
                      T.task("pe", ktr, reads=[f"kT{g4}", "ident"], writes=[PS(bank)])
                      pv = psb(bank).rearrange("p (a b) -> p a b", a=4)
                      T.task("act", lambda e: e.activation(
                          out=kf[:, g4 * 4:(g4 + 1) * 4, :], in_=pv, func=AF.Copy, scale=kdec[:, h:h + 1]),
                          reads=[PS(bank), "dectabs"], writes=["kf", PS(bank)])
                      T.task("dve", lambda e: e.tensor_scalar(
                          out=kb[:, g4 * 4:(g4 + 1) * 4, :], in0=pv, scalar1=kdec[:, 4 + h:5 + h], scalar2=None,
                          op0=ALU.mult),
                          reads=[PS(bank), "dectabs"], writes=["kb"])

                  def vproj(tb):
                      toks = slice(tb * 512, (tb + 1) * 512)
                      for dt in range(2):
                          bank = 6 + dt
                          mm(ps[bank][:], [(wh[:, 2, kt, dt * 128:(dt + 1) * 128], hxT[:, kt, toks]) for kt in range(8)],
                             reads=["wh", "hxT"], writes=[PS(bank)])
                          T.task("act", lambda e, dt=dt, bank=bank: e.activation(
                              out=zaT[:, dt, toks], in_=ps[bank][:], func=AF.Copy),
                              reads=[PS(bank)], writes=[f"zaT{tb}"])

                  def vtr_group(g4):
                      bank = 4 + g4 % 2

                      def vtr(pe):
                          ins = None
                          for t in range(4):
                              tt = g4 * 4 + t
                              for dt in range(2):
                                  ins = pe.transpose(psb(bank)[:, t * 256 + dt * 128:t * 256 + (dt + 1) * 128],
                                                     zaT[:, dt, tt * 128:(tt + 1) * 128], ident[:])
                          return ins
                      T.task("pe", vtr, reads=[f"zaT{g4}", "ident"], writes=[PS(bank)])
                      pv = psb(bank).rearrange("p (a b) -> p a b", a=4)
                      T.task("dve", lambda e: e.tensor_copy(out=vv[:, g4 * 4:(g4 + 1) * 4, :], in_=pv),
                             reads=[PS(bank)], writes=["vv"])

                  for tb in range(4):
                      qk_unit(1, tb, 2 * (tb % 2))
                      if tb >= 1:
                          ktr_group(tb - 1)
                  ck("a3")
                  vproj(0)
                  ktr_group(3)
                  for tb in range(1, 4):
                      vproj(tb)
                      vtr_group(tb - 1)
                  vtr_group(3)
                  ck("a4")

                  za_jobs = [(dt, tb) for dt in range(2) for tb in range(4)]
                  dense = [("q", tb) for tb in range(4)] + [("za", j) for j in range(len(za_jobs))]

                  def za_job(dt, tb, idx):
                      bank = 6 + idx % 2
                      toks = slice(tb * 512, (tb + 1) * 512)
                      mm(ps[bank][:], [(wh[:, 3, kt, dt * 128:(dt + 1) * 128], hxT[:, kt, toks]) for kt in range(8)],
                         reads=["wh", "hxT"], writes=[PS(bank)])
                      T.task("act", lambda e: e.activation(out=zaT[:, dt, toks], in_=ps[bank][:], func=AF.Silu),
                             reads=[PS(bank)], writes=[f"zaT{tb}"])
                  ck("p2a")
                  if False:
                      T.dma("pool", "wh", [(wh[:, s_, :, :], wh_d[h + 1, :, s_, :, :]) for s_ in range(4)],
                            writes=["wh"])

                  cur = 0
                  for c in range(NT - 1, -1, -1):
                      T.task("act", lambda e, c=c, cur=cur: e.activation(out=Tb[:, c, :, :], in_=S32[:, 1, cur, :, :],
                                                                       func=AF.Copy),
                             reads=[f"S32_1_{cur}"], writes=[f"Tb{c}"])
                      if c > 0:
                          bank = c % 2
                          mm_multi([(ps[bank][:, dt * 256:(dt + 1) * 256],
                                     [(kb[:, c, dt * 128:(dt + 1) * 128], vv[:, c, :])]) for dt in range(2)],
                                   reads=["kb", "vv"], writes=[PS(bank)])
                          T.task("dve", lambda e, bank=bank, h=h, cur=cur: e.scalar_tensor_tensor(
                              out=S32[:, 1, 1 - cur, :, :], in0=S32[:, 1, cur, :, :], scalar=gC[:, 4 + h:5 + h],
                              in1=ps[bank][:].rearrange("p (a b) -> p a b", a=2), op0=ALU.mult, op1=ALU.add),
                              reads=[PS(bank), f"S32_1_{cur}", "gC"], writes=[f"S32_1_{1 - cur}"])
                          cur = 1 - cur
                      zi = NT - 1 - c
                      if zi < len(dense):
                          if dense[zi][0] == "q":
                              qk_unit(0, dense[zi][1], 2 + 2 * (zi % 2))
                          else:
                              j = dense[zi][1]
                              za_job(za_jobs[j][0], za_jobs[j][1], j)
                          if zi == len(dense) - 1 and h + 1 < H:
                              T.dma("pool", "wh", [(wh[:, s_, :, :], wh_d[h + 1, :, s_, :, :]) for s_ in range(4)],
                                    writes=["wh"])
                          if zi == len(dense) - 1 and h + 1 == H:
                              T.dma("pool", "wh", [(whv, wB_d[0])], writes=["wh"])
                  ck("p2b")

                  def O1(c, h=h):
                      s2 = c % 2
                      ct = slice(c * 128, (c + 1) * 128)
                      cur = c % 2
                      T.task("act", lambda e: e.activation(out=Sf[s2][:], in_=S32[:, 0, cur, :, :], func=AF.Copy),
                             reads=[f"S32_0_{cur}"], writes=[f"Sf{s2}"])
                      if c < NT - 1:
                          bank = 0
                          mm_multi([(ps[bank][:, dt * 256:(dt + 1) * 256],
                                     [(kf[:, c, dt * 128:(dt + 1) * 128], vv[:, c, :])]) for dt in range(2)],
                                   reads=["kf", "vv"], writes=[PS(bank)])
                          T.task("dve", lambda e: e.scalar_tensor_tensor(
                              out=S32[:, 0, 1 - cur, :, :], in0=S32[:, 0, cur, :, :], scalar=gC[:, h:h + 1],
                              in1=ps[bank][:].rearrange("p (a b) -> p a b", a=2), op0=ALU.mult, op1=ALU.add),
                              reads=[PS(bank), f"S32_0_{cur}", "gC"], writes=[f"S32_0_{1 - cur}"])

                      def qsc(e):
                          e.tensor_tensor(out=qfb[s2][:, 0, :, :], in0=qT[:, :, ct],
                                          in1=qdec[:, h, 0, :].unsqueeze(1).broadcast_to([128, 2, 128]), op=ALU.mult)
                          return e.tensor_tensor(out=qfb[s2][:, 1, :, :], in0=qT[:, :, ct],
                                                 in1=qdec[:, h, 1, :].unsqueeze(1).broadcast_to([128, 2, 128]),
                                                 op=ALU.mult)
                      T.task("pool", qsc, reads=["qT", "dectabs"], writes=[f"qfb{s2}"])
                      sbank = 2 + c % 2
                      skey = PS(sbank)
                      sT = ps[sbank][:, 0:128]
                      mm(sT, [(kT[:, dt, ct], qT[:, dt, ct]) for dt in range(2)], reads=[f"kT{c // 4}", "qT"],
                         writes=[skey])
                      T.task("dve", lambda e: e.tensor_tensor(out=PT[s2][:], in0=sT, in1=DT[:, h, :], op=ALU.mult),
                             reads=[skey, "DT"], writes=[f"PT{s2}"])

                  def O1b(c, h=h):
                      s2 = c % 2
                      o4 = c % 4
                      obank = 4 + o4
                      okey = PS(obank)
                      oap = ps[obank][:, 0:256]
                      mm(oap, [(PT[s2][:], vv[:, c, :]),
                               (qfb[s2][:, 0, 0, :], Sf[s2][:, 0, :]), (qfb[s2][:, 0, 1, :], Sf[s2][:, 1, :]),
                               (qfb[s2][:, 1, 0, :], Tb[:, c, 0, :]), (qfb[s2][:, 1, 1, :], Tb[:, c, 1, :])],
                         reads=[f"PT{s2}", "vv", f"qfb{s2}", f"Sf{s2}", f"Tb{c}"], writes=[okey])

                  def O2(c):
                      o4 = c % 4
                      obank = 4 + o4
                      okey = PS(obank)
                      oap = ps[obank][:, 0:256]
                      T.task("dve", lambda e: e.bn_stats(gst[:, o4, :], oap), reads=[okey], writes=[f"gst{o4}"])
                      T.task("dve", lambda e: e.bn_aggr(gmv[:, o4, :], gst[:, o4, :]), reads=[f"gst{o4}"],
                             writes=[f"gmv{o4}"])
                      T.task("act", lambda e: e.activation(out=gl[:, o4:o4 + 1], in_=gmv[:, o4, 1:2], func=AF.Ln,
                                                           bias=EPS),
                             reads=[f"gmv{o4}"], writes=[f"gl{o4}"])
                      T.task("act", lambda e: e.activation(out=grs[:, o4:o4 + 1], in_=gl[:, o4:o4 + 1], func=AF.Exp,
                                                           scale=-0.5),
                             reads=[f"gl{o4}"], writes=[f"grs{o4}"])

                  def O3(c, h=h):
                      o4 = c % 4
                      obank = 4 + o4
                      okey = PS(obank)
                      oap = ps[obank][:, 0:256]
                      yslot = (c // 4) % 2
                      T.task("pool", lambda e: e.tensor_scalar(
                          out=gnm[:, o4:o4 + 1], in0=gmv[:, o4, 0:1], scalar1=grs[:, o4:o4 + 1], scalar2=-1.0,
                          op0=ALU.mult, op1=ALU.mult),
                          reads=[f"gmv{o4}", f"grs{o4}"], writes=[f"gnm{o4}"])
                      T.task("act", lambda e: e.activation(out=ynb[yslot][:, o4, :], in_=oap, func=AF.Identity,
                                                           scale=grs[:, o4:o4 + 1], bias=gnm[:, o4:o4 + 1]),
                             reads=[okey, f"grs{o4}", f"gnm{o4}"], writes=[f"ynb{yslot}"])
                      if o4 == 3:
                          c4 = c // 4
                          bank = 1

                          def ytr(pe):
                              ins = None
                              for t in range(4):
                                  for dt in range(2):
                                      ins = pe.transpose(psb(bank)[:, dt * 512 + t * 128:dt * 512 + (t + 1) * 128],
                                                         ynb[yslot][:, t, dt * 128:(dt + 1) * 128], ident[:])
                              return ins
                          T.task("pe", ytr, reads=[f"ynb{yslot}", "ident"], writes=[PS(bank)])

                          def yev(e):
                              ins = None
                              for dt in range(2):
                                  ft = h * 2 + dt
                                  ins = e.scalar_tensor_tensor(
                                      out=yaT[:, ft, c4 * 512:(c4 + 1) * 512], in0=psb(bank)[:, dt * 512:(dt + 1) * 512],
                                      scalar=colv[:, 32 + ft:33 + ft], in1=zaT[:, dt, c4 * 512:(c4 + 1) * 512],
                                      op0=ALU.mult, op1=ALU.mult)
                              return ins
                          T.task("dve", yev, reads=[PS(bank), f"zaT{c4}", "consts"], writes=["yaT"])

                  for i in range(NT + 3):
                      if i < NT:
                          O1(i)
                      if 0 <= i - 1 < NT:
                          O1b(i - 1)
                      if 0 <= i - 2 < NT:
                          O2(i - 2)
                      if 0 <= i - 3 < NT:
                          O3(i - 3)
              tap("yaT", yaT[:], [128, 8, N], ["yaT"], True)
              ck("p2")
              T.barrier()
          p02.close()

          ybT = sb(outer, "ybT", [128, 8, N], BF16)
          with contextlib.ExitStack() as p3:
              vln = sb(p3, "vln", [128, NT, D], BF16)
              wBs = [whv, sb(p3, "wBs1", [128, 8, D], BF16)]
              g32 = [sb(p3, f"g32_{i}", [128, D], F32) for i in range(2)]
              lst = sb(p3, "lst", [128, NT, 12], F32)
              lmv = sb(p3, "lmv", [128, NT, 2], F32)
              ll = sb(p3, "ll", [128, NT], F32)
              lrs = sb(p3, "lrs", [128, NT], F32)
              lnm = sb(p3, "lnm", [128, NT], F32)
              wsT = sb(p3, "wsT", [128, 8, 128], BF16)
              wsT32 = sb(p3, "wsT32", [128, 8, 128], F32)
              lnbb = sb(p3, "lnbb", [128, D], F32)
              bsb = sb(p3, "bsb", [128, 8, 128], F32)
              bias2 = sb(p3, "bias2", [128, 8, 128], F32)
              szb = [sb(p3, f"szb{i}", [128, 512], F32) for i in range(2)]
              mxb = [sb(p3, f"mxb{i}", [128, 512], F32) for i in range(2)]

              T.dma("pool", "wBs1", [(wBs[1][:], wB_d[1])], writes=["wBs1"])
              T.dma("pool", "wsT", [(wsT[:], wsT_d)], writes=["wsT"])
              T.dma("sp", "c3", [(wsT32[:], wsT_d), (lnbb[:], lnb_d.partition_broadcast(128)),
                                 (bsb[:].rearrange("p a b -> p (a b)"), bs_d.partition_broadcast(128))],
                    writes=["c3"])
              for tt in range(NT):
                  s2 = tt % 2
                  b0 = 2 * s2
                  tok = slice(tt * 128, (tt + 1) * 128)
                  mm_multi([(ps[b0 + i][:], [(hxT[:, kt, tok], wBs[0][:, kt, i * 512:(i + 1) * 512]) for kt in range(8)])
                            for i in range(2)],
                           reads=["hxT", "wh"], writes=[PS(b0), PS(b0 + 1)])

                  def gev(e, s2=s2, b0=b0):
                      e.activation(out=g32[s2][:, 0:512], in_=ps[b0][:], func=AF.Gelu_apprx_tanh)
                      return e.activation(out=g32[s2][:, 512:1024], in_=ps[b0 + 1][:], func=AF.Gelu_apprx_tanh)
                  T.task("act", gev, reads=[PS(b0), PS(b0 + 1)], writes=[f"g32_{s2}"])

                  def lstat(e, s2=s2, tt=tt):
                      e.bn_stats(lst[:, tt, 0:6], g32[s2][:, 0:512])
                      return e.bn_stats(lst[:, tt, 6:12], g32[s2][:, 512:1024])
                  T.task("dve", lstat, reads=[f"g32_{s2}"], writes=[f"lst{tt}"])
                  T.task("dve", lambda e, tt=tt: e.bn_aggr(lmv[:, tt, :], lst[:, tt, :]), reads=[f"lst{tt}"],
                         writes=["lmv"])
                  T.task("dve", lambda e, s2=s2, tt=tt: e.tensor_copy(out=vln[:, tt, :], in_=g32[s2][:]),
                         reads=[f"g32_{s2}"], writes=["vln"])
              T.task("act", lambda e: e.activation(out=ll[:], in_=lmv[:, :, 1], func=AF.Ln, bias=EPS),
                     reads=["lmv"], writes=["ll"])
              T.task("act", lambda e: e.activation(out=lrs[:], in_=ll[:], func=AF.Exp, scale=-0.5),
                     reads=["ll"], writes=["lrs"])
              T.task("dve", lambda e: e.scalar_tensor_tensor(out=lnm[:], in0=lmv[:, :, 0], scalar=-1.0, in1=lrs[:],
                                                              op0=ALU.mult, op1=ALU.mult),
                     reads=["lmv", "lrs"], writes=["lnm"])

              for t4 in range(4):
                  def vnorm(e, t4=t4):
                      ins = None
                      for tt in range(t4 * 4, t4 * 4 + 4):
                          ins = e.tensor_scalar(out=vln[:, tt, :], in0=vln[:, tt, :], scalar1=lrs[:, tt:tt + 1],
                                                scalar2=lnm[:, tt:tt + 1], op0=ALU.mult, op1=ALU.add)
                      return ins
                  T.task("dve", vnorm, reads=["vln", "lrs", "lnm"], writes=["vln"])
              T.dma("pool", "wh", [(wBs[0][:], wB_d[2])], writes=["wh"])

              n = 0
              for g in range(8):
                  for tb in range(4):
                      bank = n % 4
                      n += 1
                      toks = slice(tb * 512, (tb + 1) * 512)
                      mm(ps[bank][:], [(wBs[1][:, kt, g * 128:(g + 1) * 128], hxT[:, kt, toks]) for kt in range(8)],
                         reads=["wBs1", "hxT"], writes=[PS(bank)])
                      T.task("act", lambda e, g=g, toks=toks, bank=bank: e.activation(
                          out=ybT[:, g, toks], in_=ps[bank][:], func=AF.Gelu_apprx_tanh),
                          reads=[PS(bank)], writes=["ybT"])
              for half in range(2):
                  bank = 4 + half
                  mm_multi([(ps[bank][:, gi * 128:(gi + 1) * 128],
                             [(lnbb[:, (half * 4 + gi) * 128:(half * 4 + gi + 1) * 128], wsT32[:, half * 4 + gi, :])])
                            for gi in range(4)],
                           reads=["c3"], writes=[PS(bank)])
                  T.task("dve", lambda e, half=half, bank=bank: e.tensor_tensor(
                      out=bias2[:, half * 4:(half + 1) * 4, :], in0=ps[bank][:].rearrange("p (a b) -> p a b", a=4),
                      in1=bsb[:, half * 4:(half + 1) * 4, :], op=ALU.add),
                      reads=[PS(bank), "c3"], writes=["bias2"])
              n = 0
              for g in range(8):
                  for tb in range(4):
                      s2 = n % 2
                      n += 1
                      zbank = s2
                      mbank = 2 + s2
                      toks = slice(tb * 512, (tb + 1) * 512)
                      mm(ps[zbank][:], [(wBs[0][:, kt, g * 128:(g + 1) * 128], hxT[:, kt, toks]) for kt in range(8)],
                         reads=["wh", "hxT"], writes=[PS(zbank)])
                      mm_multi([(ps[mbank][:, cc * 128:(cc + 1) * 128],
                                 [(vln[:, tb * 4 + cc, g * 128:(g + 1) * 128], wsT[:, g, :])]) for cc in range(4)],
                               reads=["vln", "wsT"], writes=[PS(mbank)])
                      T.task("act", lambda e, s2=s2, zbank=zbank: e.activation(out=szb[s2][:], in_=ps[zbank][:],
                                                                             func=AF.Silu),
                             reads=[PS(zbank)], writes=[f"szb{s2}"])
                      T.task("dve", lambda e, s2=s2, mbank=mbank, g=g: e.scalar_tensor_tensor(
                          out=mxb[s2][:].rearrange("p (a b) -> p a b", a=4),
                          in0=ps[mbank][:].rearrange("p (a b) -> p a b", a=4), scalar=colv[:, 40 + g:41 + g],
                          in1=bias2[:, g, :].unsqueeze(1).broadcast_to([128, 4, 128]), op0=ALU.mult, op1=ALU.add),
                          reads=[PS(mbank), "bias2", "consts"], writes=[f"mxb{s2}"])

                      T.task("dve", lambda e, s2=s2: e.tensor_tensor(out=mxb[s2][:], in0=mxb[s2][:], in1=szb[s2][:],
                                                                     op=ALU.mult),
                             reads=[f"mxb{s2}", f"szb{s2}"], writes=[f"mxb{s2}"])
                      T.task("pool", lambda e, s2=s2, g=g, toks=toks: e.tensor_tensor(
                          out=ybT[:, g, toks], in0=ybT[:, g, toks], in1=mxb[s2][:], op=ALU.mult),
                          reads=[f"mxb{s2}", "ybT"], writes=["ybT"])
              tap("ybT", ybT[:], [128, 8, N], ["ybT"], True)
              ck("p3")
              T.barrier()

          with contextlib.ExitStack() as p4:
              mT = sb(p4, "mT", [128, 8, N], BF16)
              w4s = [sb(p4, f"w4s{i}", [128, 4, 8, 128], BF16) for i in range(2)]
              woutb = whv
              wst = [sb(p4, f"wst{i}", [128, D], F32) for i in range(2)]
              fgb = sb(p4, "fgb", [128, D], F32)
              sga = [sb(p4, f"sga{i}", [128, 512], F32) for i in range(2)]
              sgb = [sb(p4, f"sgb{i}", [128, 512], F32) for i in range(2)]
              t1 = sga
              t2 = sgb
              xs4 = [sb(p4, f"xs4_{i}", [128, D], F32) for i in range(4)]
              junk4 = sb(p4, "junk4", [128, D], BF16)
              ss4 = sb(p4, "ss4", [128, NT], F32)
              ls4 = sb(p4, "ls4", [128, NT], F32)
              rs4 = sb(p4, "rs4", [128, NT], F32)

              T.dma("sp", "c4", [(fgb[:], fg_d.partition_broadcast(128))], writes=["fgb"])
              for d0 in range(2):
                  T.dma("pool", f"w4s{d0}", [(w4s[d0][:], w4_d[d0])], writes=[f"w4s{d0}"])
              T.task("pool", lambda e: e.memset(ss4[:], 0.0), writes=["ss4"])
              n = 0
              for dtile in range(8):
                  ws2 = dtile % 2
                  for tb in range(4):
                      s2 = n % 2
                      n += 1
                      base = 4 * s2
                      toks = slice(tb * 512, (tb + 1) * 512)
                      srcs = (hxT, hxT, yaT, ybT)
                      keys = ("hxT", "hxT", "yaT", "ybT")
                      for m in range(4):
                          mm(ps[base + m][:], [(w4s[ws2][:, m, kt, :], srcs[m][:, kt, toks]) for kt in range(8)],
                             reads=[f"w4s{ws2}", keys[m]], writes=[PS(base + m)])

                      def sig(e, s2=s2, base=base):
                          e.activation(out=sga[s2][:], in_=ps[base][:], func=AF.Sigmoid)
                          return e.activation(out=sgb[s2][:], in_=ps[base + 1][:], func=AF.Sigmoid)
                      T.task("act", sig, reads=[PS(base), PS(base + 1)], writes=[f"sg{s2}"])

                      def mrg(e, s2=s2, base=base):
                          e.tensor_tensor(out=t1[s2][:], in0=ps[base + 2][:], in1=sga[s2][:], op=ALU.mult)
                          return e.tensor_tensor(out=t2[s2][:], in0=ps[base + 3][:], in1=sgb[s2][:], op=ALU.mult)
                      T.task("dve", mrg, reads=[PS(base + 2), PS(base + 3), f"sg{s2}"], writes=[f"sg{s2}"])
                      T.task("pool", lambda e, s2=s2, dtile=dtile, toks=toks: e.tensor_tensor(
                          out=mT[:, dtile, toks], in0=t1[s2][:], in1=t2[s2][:], op=ALU.add),
                          reads=[f"sg{s2}"], writes=["mT"])
                  if dtile + 2 < 8:
                      T.dma("pool", f"w4s{ws2}", [(w4s[ws2][:], w4_d[dtile + 2])], writes=[f"w4s{ws2}"])
                  pc = dtile
                  T.dma("sp", f"wst{pc % 2}", [(wst[pc % 2][:], wout_d[:, pc, :])], writes=[f"wst{pc % 2}"])
                  T.task("dve", lambda e, pc=pc: e.tensor_tensor(
                      out=woutb[:, pc, :], in0=wst[pc % 2][:], in1=gtb[:], op=ALU.mult),
                      reads=[f"wst{pc % 2}", "gtb"], writes=["wh"])
              tap("mT", mT[:], [128, 8, N], ["mT"], True)
              ck("p4a")
              def F1(tt):
                  s3 = tt % 4
                  b0 = 2 * (tt % 4)
                  tok = slice(tt * 128, (tt + 1) * 128)
                  if tt == 0:
                      for t0 in range(2):
                          T.dma("sp", f"xs4_{t0}", [(xs4[t0][:], x_d[t0 * 128:(t0 + 1) * 128, :])],
                                writes=[f"xs4_{t0}"])
                  if tt + 2 < NT:
                      sn = (tt + 2) % 4
                      T.dma("sp", f"xs4_{sn}", [(xs4[sn][:], x_d[(tt + 2) * 128:(tt + 3) * 128, :])],
                            writes=[f"xs4_{sn}"])
                  mm_multi([(ps[b0 + i][:], [(mT[:, kt, tok], woutb[:, kt, i * 512:(i + 1) * 512]) for kt in range(8)])
                            for i in range(2)],
                           reads=["mT", "wh"], writes=[PS(b0), PS(b0 + 1)])

                  def resid(e):
                      e.tensor_tensor(out=xs4[s3][:, 0:512], in0=ps[b0][:], in1=xs4[s3][:, 0:512], op=ALU.add)
                      return e.tensor_tensor(out=xs4[s3][:, 512:1024], in0=ps[b0 + 1][:], in1=xs4[s3][:, 512:1024],
                                             op=ALU.add)
                  T.task("dve", resid, reads=[PS(b0), PS(b0 + 1), f"xs4_{s3}"], writes=[f"xs4_{s3}"])
                  T.task("act", lambda e: e.activation(out=junk4[:], in_=xs4[s3][:], func=AF.Square,
                                                       accum_out=ss4[:, tt:tt + 1]),
                         reads=[f"xs4_{s3}", "ss4"], writes=["junk4", f"ss4_{tt}"])
                  T.task("act", lambda e: e.activation(out=ls4[:, tt:tt + 1], in_=ss4[:, tt:tt + 1], func=AF.Ln,
                                                       scale=1.0 / D, bias=EPS),
                         reads=[f"ss4_{tt}"], writes=[f"ls4_{tt}"])
                  T.task("act", lambda e: e.activation(out=rs4[:, tt:tt + 1], in_=ls4[:, tt:tt + 1],
                                                       func=AF.Exp, scale=-0.5),
                         reads=[f"ls4_{tt}"], writes=[f"rs4_{tt}"])

              def F2(tt):
                  s3 = tt % 4
                  tok = slice(tt * 128, (tt + 1) * 128)
                  T.task("dve", lambda e: e.scalar_tensor_tensor(
                      out=xs4[s3][:], in0=xs4[s3][:], scalar=rs4[:, tt:tt + 1], in1=fgb[:], op0=ALU.mult,
                      op1=ALU.mult),
                      reads=[f"xs4_{s3}", f"rs4_{tt}", "fgb"], writes=[f"xs4_{s3}"])
                  T.dma("pool", f"xs4_{s3}", [(y_d[tok, :], xs4[s3][:])], reads=[f"xs4_{s3}"])

              for i in range(NT + 1):
                  if i < NT:
                      F1(i)
                  if i >= 1:
                      F2(i - 1)
              T.finish("sp")
    except _Stop:
        T.finish("sp")
    return nc, tap_out, T


def _kt_layout(w):
    k, c = w.shape
    return np.ascontiguousarray(w.reshape(8, 128, c).transpose(1, 0, 2))


def _col_layout(v):
    return np.ascontiguousarray(v.reshape(-1, 128).T)


def _host_consts():
    f32 = np.float32
    n = np.arange(N)
    r = (n // 64).astype(np.float64)
    col = (n % 64).astype(np.float64)
    quarter = 64
    inv = np.power(10000.0, -np.arange(quarter, dtype=np.float64) / quarter)
    ang = np.concatenate([r[None, :] * inv[:, None], col[None, :] * inv[:, None]], axis=0)
    rope = np.stack([np.cos(ang), np.sin(ang)], axis=1).astype(f32)
    j = np.arange(128, dtype=f32)[:, None]
    i = np.arange(128, dtype=f32)[None, :]
    diffF = np.maximum(i - j, 0)
    diffB = np.maximum(j - i, 0)
    maskF = (i >= j).astype(f32) / 16.0
    maskB = (j > i).astype(f32) / 16.0
    cmat = np.stack([diffF, diffB, maskF, maskB], axis=1).astype(f32)
    return rope, cmat


def _prepare(inputs):
    f32 = np.float32
    g = {k: np.asarray(v, dtype=f32) for k, v in inputs.items()}
    w_in = g["w_in"][0]
    rope, cmat = _host_consts()
    shared = {}
    shared["wmod"] = _kt_layout(g["w_mod"][0])
    colv = np.zeros((128, 64), f32)
    colv[:, 0:24] = _col_layout(g["b_mod"][0])
    colv[:, 24:32] = _col_layout(g["norm_g"][0])
    colv[:, 32:40] = _col_layout(g["ret_norm_g"][0])
    colv[:, 40:48] = _col_layout(g["mlp_ln_g"][0])
    p = np.arange(128, dtype=f32)
    colv[:, 48] = 127.0 - p
    colv[:, 49] = p
    colv[:, 50] = 255.0 - p
    colv[:, 51] = 255.0 - (128.0 + p)
    colv[:, 52] = p
    colv[:, 53] = 128.0 + p
    shared["colv"] = colv
    rowA = np.zeros((1, 264), f32)
    rowA[0, 0:4] = g["ret_decay_fwd"][0]
    rowA[0, 4:8] = g["ret_decay_bwd"][0]
    rowA[0, 8:136] = np.arange(128, dtype=f32) + 1.0
    rowA[0, 136:264] = 128.0 - np.arange(128, dtype=f32)
    shared["rowA"] = rowA
    shared["bmod2"] = np.ascontiguousarray(np.stack([g["b_mod"][0], g["b_mod"][0]], axis=0))
    shared["fg"] = np.ascontiguousarray(g["final_norm_g"][None, :])
    shared["lnb"] = np.ascontiguousarray(g["mlp_ln_b"][0][None, :])
    shared["bs"] = np.ascontiguousarray(g["mlp_bs"][0].reshape(1, 8 * 128))
    shared["cmat"] = cmat
    shared["rope"] = rope
    shared["ident"] = np.eye(128, dtype=f32)
    shared["wsT"] = np.ascontiguousarray(g["mlp_ws"][0].transpose(2, 0, 1))
    wh = np.zeros((H, 128, 4, 8, 256), f32)
    for h in range(H):
        for s in range(4):
            c0 = s * 1024 + h * 256
            wh[h, :, s] = _kt_layout(w_in[:, c0:c0 + 256])
    shared["wh"] = wh
    shared["wB"] = np.stack([_kt_layout(w_in[:, 5 * 1024:6 * 1024]), _kt_layout(w_in[:, 4 * 1024:5 * 1024]),
                             _kt_layout(w_in[:, 6 * 1024:7 * 1024])], axis=0)
    w4 = np.zeros((8, 128, 4, 8, 128), f32)
    mats = (w_in[:, 7 * 1024:8 * 1024], w_in[:, 8 * 1024:9 * 1024], g["w_proj_a"][0], g["w_proj_b"][0])
    for dt in range(8):
        for m in range(4):
            w4[dt, :, m] = _kt_layout(mats[m][:, dt * 128:(dt + 1) * 128])
    shared["w4"] = w4
    shared["wout"] = _kt_layout(g["w_out"][0])
    in_maps = []
    for b in range(8):
        d = dict(shared)
        d["x"] = np.ascontiguousarray(g["x"][b])
        d["ctx"] = np.ascontiguousarray(g["ctx"][b])
        cc = np.stack([g["c"][b], g["c_ctx"]], axis=-1)
        d["cT"] = np.ascontiguousarray(cc.reshape(8, 128, 2).transpose(1, 0, 2))
        in_maps.append(d)
    return in_maps


def kernel(**inputs):
    in_maps = _prepare(inputs)
    nc, _, _ = build()
    res = run_bass_kernel_spmd(nc, in_maps, core_ids=list(range(8)))
    out = np.stack([np.asarray(res.results[b]["y"], dtype=np.float32) for b in range(8)], axis=0)
    return out
```
